# Optimizing a Trainium2 kernel written in Bass

```python
import jax, jax.numpy as jnp
from jax import lax
import numpy as np

D_MODEL = 1024
BATCH = 16
SEQ = 2048
DEPTH = 1
DEC_BATCH = 32
DEC_SEQ = 32
PAST_LEN = 1024

CHUNK = 64
D_MIX = D_MODEL
D_CONV = D_MIX // 2
D_ATT = D_MIX - D_CONV
CONV_WIDTH = 31
N_HEADS = 8
QK_NOPE = 64
QK_ROPE = 32
V_DIM = D_ATT // N_HEADS
Q_LORA = 256
KV_LORA = 128
ROPE_THETA = 10000.0
EPS = 1e-6
Q_BLOCK = 128
ATT_SCALE = (QK_NOPE + QK_ROPE) ** -0.5
NEG = -1e30
SPLITS = list(np.cumsum([D_CONV, D_CONV, D_CONV, Q_LORA, KV_LORA, QK_ROPE])[:].tolist())
D_IN = 3 * D_CONV + Q_LORA + KV_LORA + QK_ROPE + D_ATT

kernel_name = "hymba_conformer_mla_stream_step"


def _rmsnorm(x, g):
    xf = x.astype(jnp.float32)
    y = xf * lax.rsqrt(jnp.mean(xf * xf, axis=-1, keepdims=True) + EPS)
    return (y * g.astype(jnp.float32)).astype(x.dtype)


def _layernorm(x, g, b):
    xf = x.astype(jnp.float32)
    mu = jnp.mean(xf, axis=-1, keepdims=True)
    var = jnp.mean(jnp.square(xf - mu), axis=-1, keepdims=True)
    y = (xf - mu) * lax.rsqrt(var + EPS) * g.astype(jnp.float32) + b.astype(jnp.float32)
    return y.astype(x.dtype)


def _rope_tables(pos):
    inv = ROPE_THETA ** (-jnp.arange(0, QK_ROPE, 2, dtype=jnp.float32) / QK_ROPE)
    ang = pos.astype(jnp.float32)[:, None] * inv[None, :]
    return jnp.cos(ang), jnp.sin(ang)


def _apply_rope(x, cos, sin):
    x1, x2 = jnp.split(x.astype(jnp.float32), 2, axis=-1)
    return jnp.concatenate([x1 * cos - x2 * sin, x2 * cos + x1 * sin], axis=-1).astype(x.dtype)


def _scores(q_nope, q_rope, k_nope, k_rope):
    s = jnp.einsum('bqhd,bkhd->bhqk', q_nope, k_nope, preferred_element_type=jnp.float32)
    s = s + jnp.einsum('bqhr,bkr->bhqk', q_rope, k_rope, preferred_element_type=jnp.float32)
    return s * ATT_SCALE


def _attend_prompt(q_nope, q_rope, k_nope, k_rope, v):
    B, S = q_nope.shape[:2]
    nb = S // Q_BLOCK

    def blocks(t):
        return jnp.moveaxis(t.reshape((B, nb, Q_BLOCK) + t.shape[2:]), 1, 0)

    k_chunk = jnp.arange(S) // CHUNK

    def one(args):
        qn, qr, i = args
        q_chunk = (i * Q_BLOCK + jnp.arange(Q_BLOCK)) // CHUNK
        mask = k_chunk[None, :] <= q_chunk[:, None]
        s = jnp.where(mask[None, None], _scores(qn, qr, k_nope, k_rope), NEG)
        p = jax.nn.softmax(s, axis=-1).astype(v.dtype)
        return jnp.einsum('bhqk,bkhd->bqhd', p, v)

    o = lax.map(one, (blocks(q_nope), blocks(q_rope), jnp.arange(nb)))
    return jnp.moveaxis(o, 0, 1).reshape(B, S, N_HEADS * V_DIM)


def _attend_sample(q_nope, q_rope, k_nope, k_rope, v):
    B, T = q_nope.shape[:2]
    p = jax.nn.softmax(_scores(q_nope, q_rope, k_nope, k_rope), axis=-1).astype(v.dtype)
    return jnp.einsum('bhqk,bkhd->bqhd', p, v).reshape(B, T, N_HEADS * V_DIM)


def _conv_branch(glu, hist, conv_w, conv_b, ln_g, ln_b):
    xin = jnp.concatenate([hist, glu], axis=1)
    y = lax.conv_general_dilated(
        xin, conv_w[:, None, :], window_strides=(1,), padding='VALID',
        dimension_numbers=('NWC', 'WIO', 'NWC'), feature_group_count=D_CONV) + conv_b
    y = jax.nn.silu(_layernorm(y, ln_g, ln_b))
    return y, xin[:, -(CONV_WIDTH - 1):]


def _mixer_layer(x, pos, conv_hist, ckv_past, krope_past, g_pre, w_in, conv_w, conv_b,
                 conv_ln_g, conv_ln_b, g_qa, w_qb, g_kva, w_kvb, w_out, g_post):
    B, T = x.shape[:2]
    prompt = ckv_past is None
    h = _rmsnorm(x, g_pre)
    u = h @ w_in
    a, b, gate_c, q_c, kv_c, k_r, gate_a = jnp.split(u, SPLITS, axis=-1)

    glu = a * jax.nn.sigmoid(b)
    if conv_hist is None:
        conv_hist = jnp.zeros((B, CONV_WIDTH - 1, D_CONV), dtype=glu.dtype)
    y_conv, new_hist = _conv_branch(glu, conv_hist, conv_w, conv_b, conv_ln_g, conv_ln_b)

    cos, sin = _rope_tables(pos)
    q = (_rmsnorm(q_c, g_qa) @ w_qb).reshape(B, T, N_HEADS, QK_NOPE + QK_ROPE)
    q_nope = q[..., :QK_NOPE]
    q_rope = _apply_rope(q[..., QK_NOPE:], cos[None, :, None, :], sin[None, :, None, :])
    ckv = _rmsnorm(kv_c, g_kva)
    krope = _apply_rope(k_r, cos[None], sin[None])
    if prompt:
        ckv_all, krope_all = ckv, krope
    else:
        ckv_all = jnp.concatenate([ckv_past, ckv], axis=1)
        krope_all = jnp.concatenate([krope_past, krope], axis=1)
    L = ckv_all.shape[1]
    kv = (ckv_all @ w_kvb).reshape(B, L, N_HEADS, QK_NOPE + V_DIM)
    k_nope, v = kv[..., :QK_NOPE], kv[..., QK_NOPE:]
    if prompt:
        y_att = _attend_prompt(q_nope, q_rope, k_nope, krope_all, v)
    else:
        y_att = _attend_sample(q_nope, q_rope, k_nope, krope_all, v)

    mixed = jnp.concatenate([y_conv * jax.nn.silu(gate_c), y_att * jax.nn.silu(gate_a)], axis=-1) @ w_out
    return x + _rmsnorm(mixed, g_post), ckv, krope, new_hist


def setup_inputs(seed: int = 0) -> dict:
    key = jax.random.key(seed)
    ks = jax.random.split(key, 24)
    f32 = jnp.float32

    def nrm(k, shape, scale):
        return jax.random.normal(k, shape, f32) * scale

    def gain(k, n):
        return 1.0 + 0.01 * jax.random.normal(k, (DEPTH, n), f32)

    return {
        "x_prompt": nrm(ks[0], (BATCH, SEQ, D_MODEL), 1.0),
        "x_sample": nrm(ks[1], (DEC_BATCH, DEC_SEQ, D_MODEL), 1.0),
        "cache_ckv": nrm(ks[2], (DEPTH, DEC_BATCH, PAST_LEN, KV_LORA), 1.0),
        "cache_krope": nrm(ks[3], (DEPTH, DEC_BATCH, PAST_LEN, QK_ROPE), 1.0),
        "state_conv": nrm(ks[4], (DEPTH, DEC_BATCH, CONV_WIDTH - 1, D_CONV), 0.5),
        "g_pre": gain(ks[5], D_MODEL),
        "w_in": nrm(ks[6], (DEPTH, D_MODEL, D_IN), D_MODEL ** -0.5),
        "conv_w": nrm(ks[7], (DEPTH, CONV_WIDTH, D_CONV), CONV_WIDTH ** -0.5),
        "conv_b": nrm(ks[8], (DEPTH, D_CONV), 0.01),
        "conv_ln_g": gain(ks[9], D_CONV),
        "conv_ln_b": nrm(ks[10], (DEPTH, D_CONV), 0.01),
        "g_qa": gain(ks[11], Q_LORA),
        "w_qb": nrm(ks[12], (DEPTH, Q_LORA, N_HEADS * (QK_NOPE + QK_ROPE)), Q_LORA ** -0.5),
        "g_kva": gain(ks[13], KV_LORA),
        "w_kvb": nrm(ks[14], (DEPTH, KV_LORA, N_HEADS * (QK_NOPE + V_DIM)), KV_LORA ** -0.5),
        "w_out": nrm(ks[15], (DEPTH, D_MIX, D_MODEL), D_MIX ** -0.5),
        "g_post": gain(ks[16], D_MODEL),
    }


def reference(x_prompt, x_sample, cache_ckv, cache_krope, state_conv, g_pre, w_in, conv_w, conv_b,
              conv_ln_g, conv_ln_b, g_qa, w_qb, g_kva, w_kvb, w_out, g_post):
    pos_p = jnp.arange(x_prompt.shape[1])
    pos_s = PAST_LEN + jnp.arange(x_sample.shape[1])
    yp, ys = x_prompt, x_sample
    ckv_p, kr_p, cv_p, ckv_s, kr_s, cv_s = [], [], [], [], [], []
    for l in range(DEPTH):
        w = (g_pre[l], w_in[l], conv_w[l], conv_b[l], conv_ln_g[l], conv_ln_b[l],
             g_qa[l], w_qb[l], g_kva[l], w_kvb[l], w_out[l], g_post[l])
        yp, c1, r1, h1 = _mixer_layer(yp, pos_p, None, None, None, *w)
        ys, c2, r2, h2 = _mixer_layer(ys, pos_s, state_conv[l], cache_ckv[l], cache_krope[l], *w)
        ckv_p.append(c1); kr_p.append(r1); cv_p.append(h1)
        ckv_s.append(c2); kr_s.append(r2); cv_s.append(h2)
    new_ckv_prompt = jnp.stack(ckv_p)
    new_krope_prompt = jnp.stack(kr_p)
    new_conv_prompt = jnp.stack(cv_p)
    new_ckv_sample = jnp.stack(ckv_s)
    new_krope_sample = jnp.stack(kr_s)
    new_conv_sample = jnp.stack(cv_s)
    return (yp, ys, new_ckv_prompt, new_krope_prompt, new_conv_prompt,
            new_ckv_sample, new_krope_sample, new_conv_sample)
```

```python
import numpy as np
import sys as _sys
from contextlib import ExitStack
import concourse.bass as bass
import concourse.mybir as mybir
from concourse.bass_utils import run_bass_kernel_spmd

F32 = mybir.dt.float32
BF16 = mybir.dt.bfloat16
ALU = mybir.AluOpType
AF = mybir.ActivationFunctionType

NCORES = 8
D_MODEL = 1024
SEQ = 2048
PAST = 1024
DEC_SEQ = 32
NH = 8
EPS = 1e-6
ATT_SCALE = 96.0 ** -0.5
TILE = 512
WINDOW = 400
XLAT = 220.0
USE_BLEVEL = True
SAMPLE_POS = 'end'
KV_PSB = False
PRIO_NOISE = 0.0
PRIO_SEED = 0
DMA_ISSUE_NS = 500.0
ENG_WINDOW = {}
import os as _os0
STRICT = True
PE_RELAX = True
NSEQ_P = 2
NSEQ_S = 4


class Op:
    __slots__ = ("eng", "fn", "idx", "deps", "dma", "semkey", "signal", "val", "waits", "gidx", "cost", "fin", "done",
                 "succ", "npend", "ready", "tag", "blevel")

    def __init__(self, eng, fn, idx, dma, semkey):
        self.eng, self.fn, self.idx, self.dma, self.semkey = eng, fn, idx, dma, semkey
        self.cost = 500.0
        self.fin = 0.0
        self.done = False
        self.deps = {}
        self.signal = False
        self.val = 0
        self.waits = []


class Sched:
    ENGS = ("pe", "act", "dve", "pool", "sp")

    def __init__(self):
        self.ops = {e: [] for e in self.ENGS}
        self.lastw = {}
        self.readers = {}
        self.dma_count = {}
        self.gcount = 0

    def add(self, eng, fn, reads=(), writes=(), dma=False, semkey=None, cost=None, n=512):
        op = Op(eng, fn, len(self.ops[eng]), dma, semkey)
        op.gidx = self.gcount
        self.gcount += 1
        op.tag = _sys._getframe(1).f_lineno if not dma else _sys._getframe(2).f_lineno
        if cost is None:
            if dma:
                cost = 3500.0
            elif eng == "act":
                cost = 230.0 + 0.85 * n
            elif eng == "dve":
                cost = 120.0 + 1.05 * n
            elif eng == "pool":
                cost = 350.0 + 1.7 * n
            else:
                cost = 60.0 + 0.45 * n
        op.cost = float(cost)
        for k in reads:
            w = self.lastw.get(k)
            if w is not None:
                op.deps[w] = "raw"
            if k.startswith("ps"):
                for r in self.readers.get(k, ()):
                    if r.eng != eng and r not in op.deps:
                        op.deps[r] = "rr"
        for k in writes:
            w = self.lastw.get(k)
            if w is not None and w not in op.deps:
                op.deps[w] = "waw"
            for r in self.readers.get(k, ()):
                if r is not op and r not in op.deps:
                    op.deps[r] = "war"
        for k in reads:
            self.readers.setdefault(k, []).append(op)
        for k in writes:
            self.lastw[k] = op
            self.readers[k] = []
        if dma:
            op.signal = True
        self.ops[eng].append(op)
        return op

    def reorder(self, window=WINDOW, xlat=XLAT):
        allops = [op for e in self.ENGS for op in self.ops[e]]
        for op in allops:
            op.succ = []
            op.done = False
        for op in allops:
            op.npend = len(op.deps)
            for d in op.deps:
                d.succ.append(op)
            op.ready = 0.0
        order = sorted(allops, key=lambda o: o.gidx)
        for op in reversed(order):
            bl = 0.0
            for sc in op.succ:
                if sc.blevel > bl:
                    bl = sc.blevel
            op.blevel = bl + (op.cost if not op.dma else op.cost)
        if PRIO_NOISE > 0:
            import random as _rnd
            rng = _rnd.Random(PRIO_SEED)
            for op in order:
                op.blevel *= (1.0 + PRIO_NOISE * (rng.random() - 0.5))
        pending = {e: list(self.ops[e]) for e in self.ENGS}
        free = {e: 0.0 for e in self.ENGS}
        new = {e: [] for e in self.ENGS}
        remaining = len(allops)
        while remaining:
            best = None
            for e in self.ENGS:
                lst = pending[e]
                fe = free[e]
                cand = None
                for op in lst[:(ENG_WINDOW.get(e, window))]:
                    if op.npend:
                        continue
                    st_ = op.ready if op.ready > fe else fe
                    if cand is None:
                        cand = (st_, op)
                    else:
                        cs, cop = cand
                        if st_ <= fe and cs <= fe:
                            if USE_BLEVEL and op.blevel > cop.blevel:
                                cand = (st_, op)
                        elif st_ < cs:
                            cand = (st_, op)
                if cand is None:
                    continue
                st_, op = cand
                if best is None or st_ < best[0] - 1e-9 or (abs(st_ - best[0]) <= 1e-9 and op.gidx < best[1].gidx):
                    best = (st_, op)
            st_, op = best
            e = op.eng
            pending[e].remove(op)
            new[e].append(op)
            if op.dma:
                free[e] = st_ + DMA_ISSUE_NS
                op.fin = st_ + op.cost
            else:
                free[e] = st_ + op.cost
                op.fin = free[e]
            op.done = True
            for sc in op.succ:
                sc.npend -= 1
                r = op.fin + (xlat if (sc.eng != e or op.dma) else 0.0)
                if r > sc.ready:
                    sc.ready = r
            remaining -= 1
        self.ops = new
        for e in self.ENGS:
            for i, op in enumerate(new[e]):
                op.idx = i
        self.makespan = max(op.fin for op in allops)

    def resolve(self):
        self.reorder()
        for op in self.ops["sp"]:
            n = self.dma_count.get(op.semkey, 0) + 1
            self.dma_count[op.semkey] = n
            op.val = 16 * n
        for eng in self.ENGS:
            for op in self.ops[eng]:
                for d, kind in op.deps.items():
                    if d.dma:
                        op.waits.append(d)
                        continue
                    if d.eng == op.eng:
                        if op.dma:
                            continue
                        if STRICT and not (eng == "pe" and PE_RELAX):
                            d.signal = True
                            op.waits.append(d)
                            continue
                        if eng == "pe":
                            continue
                        if kind == "raw" and (op.idx - d.idx) <= 2:
                            d.signal = True
                            op.waits.append(d)
                        continue
                    d.signal = True
                    op.waits.append(d)
        for eng in self.ENGS:
            c = 0
            for op in self.ops[eng]:
                if op.dma:
                    continue
                if op.signal:
                    c += 1
                    op.val = c

    def emit(self, nc, engines, sems, dma_sems):
        for eng in self.ENGS:
            pass

    def run_engine(self, eng, e, sems, dma_sems, final_wait=False):
        waited = {}
        for op in self.ops[eng]:
            need = {}
            for d in op.waits:
                if d.dma:
                    key = ("dma", d.semkey)
                    sem = dma_sems[d.semkey]
                else:
                    key = ("eng", d.eng)
                    sem = sems[d.eng]
                if key not in need or need[key][1] < d.val:
                    need[key] = (sem, d.val)
            for key, (sem, val) in need.items():
                if waited.get(key, 0) >= val:
                    continue
                e.wait_ge(sem, val)
                waited[key] = val
            ins = op.fn(e)
            if op.dma:
                ins.then_inc(dma_sems[op.semkey], 16)
            elif op.signal:
                ins.then_inc(sems[eng], 1)
        if final_wait:
            for k, n in self.dma_count.items():
                e.wait_ge(dma_sems[k], 16 * n)


def build_program(do_prompt=True, debug=False):
    nc = bass.Bass("TRN2", target_bir_lowering=False, dynamic_dma_scratch_size=256)
    S = Sched()

    def din(name, shape, dt=F32):
        return nc.dram_tensor(name, list(shape), dt, kind="ExternalInput").ap()

    def dout(name, shape, dt=F32):
        return nc.dram_tensor(name, list(shape), dt, kind="ExternalOutput").ap()

    xp = din("xp", [NSEQ_P, SEQ, D_MODEL])
    xs = din("xs", [128, D_MODEL])
    cckv = din("cckv", [NSEQ_S, PAST, 128])
    ckr = din("ckr", [NSEQ_S, PAST, 32])
    sconv = din("sconv", [NSEQ_S, 30, 512])
    wbig = din("wbig", [128, 8, 2048])
    wsmall = din("wsmall", [128, 8, 448])
    wqb = din("wqb", [128, 2, 1536])
    wkv = din("wkv", [128, 1024])
    wout = din("wout", [128, 8, 1024])
    convwp = din("convwp", [128, 4, 4, 8])
    estack = din("estack", [128, 32])
    vecs = din("vecs", [128, 32])
    gpost_b = din("gpost_b", [128, 1024])
    gkva_b = din("gkva_b", [128, 128])
    ident = din("ident", [128, 128])
    costm_d = din("costm", [128, 17, 32])
    sintm_d = din("sintm", [128, 17, 32])
    cosT_d = din("cosT", [32, SEQ + 128])
    sinT_d = din("sinT", [32, SEQ + 128])

    yp = dout("yp", [NSEQ_P, SEQ, D_MODEL])
    ys = dout("ys", [128, D_MODEL])
    ockv_p = dout("ockv_p", [NSEQ_P, SEQ, 128])
    okr_p = dout("okr_p", [NSEQ_P, SEQ, 32])
    oconv_p = dout("oconv_p", [NSEQ_P, 30, 512])
    ockv_s = dout("ockv_s", [128, 128])
    okr_s = dout("okr_s", [128, 32])
    oconv_s = dout("oconv_s", [NSEQ_S, 30, 512])

    wbigD = nc.dram_tensor("wbigD", [16, 128, 1024], BF16, kind="Internal").ap()
    gluD = nc.dram_tensor("gluD", [4, 128, 544], BF16, kind="Internal").ap()

    es = ExitStack()

    def sb(name, shape, dt):
        return es.enter_context(nc.sbuf_tensor("s_" + name, list(shape), dt))

    def ps(name, shape, dt):
        return es.enter_context(nc.psum_tensor("p_" + name, list(shape), dt))

    Wsmall = sb("Wsmall", [128, 8, 448], BF16)
    Wqb = sb("Wqb", [128, 2, 1536], BF16)
    Wkv = sb("Wkv", [128, 1024], BF16)
    Wout = sb("Wout", [128, 8, 1024], BF16)
    KT = sb("KT", [128, 8, 2112], BF16)
    V = sb("V", [128, 18, 512], BF16)
    vones = sb("vones", [128, 64], BF16)
    gpostb = sb("gpostb", [128, 1024], F32)
    gkvab = sb("gkvab", [128, 128], F32)
    identf = sb("identf", [128, 128], F32)
    identb = sb("identb", [128, 128], BF16)
    onesS = sb("onesS", [128, 128], BF16)
    neghalf = sb("neghalf", [128, 512], F32)
    costm = sb("costm", [128, 17, 32], F32)
    sintm = sb("sintm", [128, 17, 32], F32)
    vec = sb("vec", [128, 32], F32)
    vech = sb("vech", [128, 32], F32)
    convwt = sb("convwt", [128, 4, 4, 8], F32)
    estk = sb("estk", [128, 32], F32)
    Wp = sb("Wp", [128, 4, 4, 8, 32], BF16)
    NWB = 3
    Wb = [sb(f"Wb{i}", [128, 8, 128], BF16) for i in range(NWB)]
    NGT = 2
    Gt = [sb(f"Gt{i}", [128, 4, 520], BF16) for i in range(NGT)]
    NXS = 2
    xsb = [sb(f"xsb{i}", [128, 1024], F32) for i in range(NXS)]
    hb = [sb(f"hb{i}", [128, 1024], BF16) for i in range(2)]
    hT = sb("hT", [128, 8, 512], BF16)
    st = sb("st", [128, 16], F32)
    qcn = [sb(f"qcn{i}", [128, 256], BF16) for i in range(2)]
    ckvsb = [sb(f"ckvsb{i}", [128, 128], F32) for i in range(2)]
    krt = [sb(f"krt{i}", [128, 96], F32) for i in range(2)]
    rt = [sb(f"rt{i}", [128, 32], F32) for i in range(2)]
    qcT = sb("qcT", [128, 2, 512], BF16)
    ckvT = sb("ckvT", [128, 512], BF16)
    krT = sb("krT", [128, 512], BF16)
    glu = sb("glu", [128, 4, 544], BF16)
    glu32 = sb("glu32", [128, 4, 128], F32)
    sg = sb("sg", [128, 4, 512], BF16)
    sga = sb("sga", [128, 4, 512], BF16)
    xres = [sb(f"xres{i}", [128, 1024], F32) for i in range(2)]
    junk = sb("junk", [128, 1024], BF16)
    rlb = sb("rlb", [128, 512], F32)
    o1b = sb("o1b", [128, 512], F32)
    ybf = sb("ybf", [128, 4, 512], BF16)
    ysq = [sb(f"ysq{i}", [128, 512], BF16) for i in range(2)]
    rstdln = sb("rstdln", [128, 512], F32)
    NPOOL = 5
    fpool = [sb(f"fp{i}", [128, 512], F32) for i in range(NPOOL)]
    mixedT = sb("mixedT", [128, 8, 512], BF16)
    QT = sb("QT", [128, 8, 512], BF16)
    cosT = sb("cosT", [128, 512], F32)
    sinT = sb("sinT", [128, 512], F32)
    NPT = 3
    PT = [sb(f"PT{i}", [128, 512], BF16) for i in range(NPT)]
    tmpo = sb("tmpo", [128, 1024], F32)
    gluS = sb("gluS", [128, 4, NSEQ_S, 64], BF16)
    osb = sb("osb", [128, 512], F32)
    psT = ps("psT", [128, 1024], BF16)
    psS = ps("psS", [128, 512], F32)
    psB = [ps(f"psB{i}", [128, 512], F32) for i in range(2)]
    psA = [ps(f"psA{i}", [128, 512], F32) for i in range(2)]
    psO = ps("psO", [128, 1024], F32)

    cnt = {"B": 0, "A": 0, "fp": 0, "xs": 0, "h": 0, "PT": 0, "Wb": 0, "Dc": 0, "ysq": 0, "tm": 0, "xr": 0, "G": 0}

    def nxt(kind, n):
        i = cnt[kind] % n
        cnt[kind] += 1
        return i

    def nB():
        i = nxt("B", 2)
        return psB[i], f"psB{i}"

    psA3 = [psA[0], psA[1], psS]

    def nA():
        i = nxt("A", 3)
        return psA3[i], f"psA{i}"

    def nfp():
        i = nxt("fp", NPOOL)
        return fpool[i], f"fp{i}"

    A = S.add

    def dma(fn, reads, writes, semkey, cost=None):
        return S.add("sp", fn, reads, writes, dma=True, semkey=semkey, cost=cost)

    dma(lambda e: e.dma_start(out=vec[:], in_=vecs[:, :]), [], ["vec"], "c_vec")
    dma(lambda e: e.dma_start(out=identf[:], in_=ident[:, :]), [], ["identf"], "c_id")
    dma(lambda e: e.dma_start(out=convwt[:], in_=convwp[:, :, :, :]), [], ["convwt"], "c_cw")
    dma(lambda e: e.dma_start(out=estk[:], in_=estack[:, :]), [], ["estk"], "c_es")
    GLUK = [f"glu{c_}" for c_ in range(4)]
    A("pool", lambda e: e.memset(glu[:, :, 542:544], 0.0), [], GLUK)
    A("pool", lambda e: e.memset(gluS[:, :, :, 62:64], 0.0), [], GLUK)
    dma(lambda e: e.dma_start(out=gpostb[:], in_=gpost_b[:, :]), [], ["gpostb"], "c_gp")
    dma(lambda e: e.dma_start(out=gkvab[:], in_=gkva_b[:, :]), [], ["gkvab"], "c_gk")
    dma(lambda e: e.dma_start(out=costm[:], in_=costm_d[:, :, :]), [], ["costm"], "c_ct")
    dma(lambda e: e.dma_start(out=sintm[:], in_=sintm_d[:, :, :]), [], ["sintm"], "c_st")

    A("dve", lambda e: e.tensor_scalar(out=vech[:], in0=vec[:], scalar1=0.5, scalar2=None, op0=ALU.mult),
      ["vec"], ["vech"])
    A("dve", lambda e: e.tensor_copy(out=identb[:], in_=identf[:]), ["identf"], ["identb"])
    A("pool", lambda e: e.memset(onesS[:], 1.0 / 512.0), [], ["onesS"])
    A("pool", lambda e: e.memset(neghalf[:], -0.5), [], ["neghalf"])
    A("pool", lambda e: e.memset(vones[:], 1.0), [], ["vones"])
    A("pool", lambda e: e.memset(krt[0][:], 0.0), [], ["krt0"])
    A("pool", lambda e: e.memset(krt[1][:], 0.0), [], ["krt1"])

    cast_engs = ["dve", "act"]
    ccnt = [0]

    def cast_scaled(out_ap, in_ap, scal_ap, reads, writes, n=1024):
        eng = cast_engs[ccnt[0] % 2]
        ccnt[0] += 1
        if eng == "act":
            if scal_ap is None:
                A("act", lambda e: e.activation(out=out_ap, in_=in_ap, func=AF.Copy), reads, writes, n=n)
            else:
                A("act", lambda e: e.activation(out=out_ap, in_=in_ap, func=AF.Copy, scale=scal_ap), reads, writes, n=n)
        else:
            if scal_ap is None:
                A(eng, lambda e: e.tensor_copy(out=out_ap, in_=in_ap), reads, writes, cost=2100.0 * n / 1024)
            else:
                A(eng, lambda e: e.tensor_scalar(out=out_ap, in0=in_ap, scalar1=scal_ap, scalar2=None,
                                                 op0=ALU.mult), reads, writes, cost=2100.0 * n / 1024)

    stg = [(xres[0], ["xres0"]), (xres[1], ["xres1"]), (tmpo, ["tmpo0", "tmpo1"])]
    stgb = [(mixedT[:, 2 * i:2 * i + 2, :].rearrange("p a b -> p (a b)"), [f"mixedT{2 * i}", f"mixedT{2 * i + 1}"]) for i in range(4)]
    scnt = {"f": 0, "b": 0}

    def nstg():
        i = scnt["f"] % 3
        scnt["f"] += 1
        return stg[i][0], stg[i][1], i

    def nstgb():
        i = scnt["b"] % 4
        scnt["b"] += 1
        return stgb[i][0], stgb[i][1], i

    for piece in range(4):
        sf, kf, si = nstg()
        dma(lambda e, sf=sf, piece=piece: e.dma_start(out=sf[:, 0:896].rearrange("p (k n) -> p k n", k=2),
                                                        in_=wsmall[:, 2 * piece:2 * piece + 2, :]),
            [], kf, f"stg{si}")
        for j in range(2):
            kc = 2 * piece + j
            cast_scaled(Wsmall[:, kc, :], sf[:, j * 448:(j + 1) * 448], vec[:, kc:kc + 1],
                        kf + ["vec"], [f"Wsmall{kc}"], n=448)
    WSMALL_KEYS = [f"Wsmall{kc}" for kc in range(8)]
    WQB_KEYS = [f"Wqb{a_}{b_}" for a_ in range(2) for b_ in range(2)]
    WOUT_KEYS = [f"Wout{kc}" for kc in range(8)]

    def setup_rest():
        for half in range(2):
            for kc in range(8):
                sf, kf, si = nstg()
                sbf, kbf, bi = nstgb()
                dma(lambda e, sf=sf, kc=kc, half=half: e.dma_start(out=sf[:, :], in_=wbig[:, kc, half * 1024:(half + 1) * 1024]),
                    [], kf, f"stg{si}")
                cast_scaled(sbf, sf[:, :], vech[:, kc:kc + 1], kf + ["vech"], kbf)
                dma(lambda e, sbf=sbf, kc=kc, half=half: e.dma_start(
                    out=wbigD[half * 8:(half + 1) * 8, :, kc * 128:(kc + 1) * 128].rearrange("c p n -> p c n"),
                    in_=sbf.rearrange("p (c n) -> p c n", c=8)),
                    kbf, [f"wbigD{half}_{kc}"], f"stgbst{bi}")
        for c in range(4):
            for g in range(4):
                for r in range(8):
                    A("dve", lambda e, c=c, g=g, r=r: e.tensor_scalar(out=Wp[:, c, g, r, :], in0=estk[:], scalar1=convwt[:, c, g, r:r + 1],
                                                                      scalar2=None, op0=ALU.mult),
                      ["estk", "convwt"], [f"Wp{c}"], n=32)
        for kc2 in range(2):
            for half in range(2):
                sf, kf, si = nstg()
                dma(lambda e, sf=sf, kc2=kc2, half=half: e.dma_start(out=sf[:, 0:768], in_=wqb[:, kc2, half * 768:(half + 1) * 768]),
                    [], kf, f"stg{si}")
                cast_scaled(Wqb[:, kc2, half * 768:(half + 1) * 768], sf[:, 0:768], vec[:, 8 + kc2:9 + kc2],
                            kf + ["vec"], [f"Wqb{kc2}{half}"], n=768)
        sf, kf, si = nstg()
        dma(lambda e, sf=sf: e.dma_start(out=sf[:, :], in_=wkv[:, :]), [], kf, f"stg{si}")
        cast_scaled(Wkv[:], sf[:, :], None, kf, ["Wkv"])
        for kc in range(8):
            sf, kf, si = nstg()
            dma(lambda e, sf=sf, kc=kc: e.dma_start(out=sf[:, :], in_=wout[:, kc, :]), [], kf, f"stg{si}")
            cast_scaled(Wout[:, kc, :], sf[:, :], None, kf, [f"Wout{kc}"])

    def rms_rstd(src_ap, src_keys, width, col, junk_ap, junk_key):
        sc = float(width) ** -0.5
        A("act", lambda e: e.activation(out=junk_ap, in_=src_ap, func=AF.Square, scale=sc,
                                        accum_out=st[:, col:col + 1]),
          src_keys, [junk_key, f"st{col}"], n=width)
        A("pool", lambda e: e.tensor_scalar(out=st[:, col:col + 1], in0=st[:, col:col + 1], scalar1=EPS,
                                            scalar2=None, op0=ALU.add),
          [f"st{col}"], [f"st{col}"], cost=300)
        A("pool", lambda e: e.tensor_tensor(out=st[:, col:col + 1], in0=st[:, col:col + 1],
                                            in1=neghalf[:, 0:1], op=ALU.pow),
          [f"st{col}", "neghalf"], [f"st{col}"], cost=550)

    def stage_AB(x_src, nsub, tm_idx, ckv_dst, kr_dst):
        for sub in range(nsub):
            ts = slice(sub * 128, (sub + 1) * 128)
            xi = nxt("xs", NXS)
            hi = nxt("h", 2)
            dma(lambda e, xi=xi, sub=sub: e.dma_start(out=xsb[xi][:], in_=x_src(sub)), [], [f"xsb{xi}"], f"xsb{xi}")
            rms_rstd(xsb[xi][:], [f"xsb{xi}"], 1024, 0, hb[hi][:], f"hb{hi}")
            A("dve", lambda e, xi=xi, hi=hi: e.tensor_scalar(out=hb[hi][:], in0=xsb[xi][:], scalar1=st[:, 0:1],
                                                            scalar2=None, op0=ALU.mult),
              [f"xsb{xi}", "st0"], [f"hb{hi}"], n=1024)

            def tr(e, hi=hi):
                ins = None
                for k in range(8):
                    ins = e.transpose(out=psT[:, k * 128:(k + 1) * 128], in_=hb[hi][:, k * 128:(k + 1) * 128],
                                      identity=identb[:])
                return ins
            A("pe", tr, [f"hb{hi}", "identb"], ["psT"], cost=700)
            A("dve", lambda e, ts=ts: e.tensor_copy(out=hT[:, :, ts], in_=psT[:].rearrange("p (k n) -> p k n", k=8)),
              ["psT"], ["hT"], n=1024)

            pSm, kSm = nB()

            def smallp(e, ts=ts, pSm=pSm):
                ins = None
                for k in range(8):
                    ins = e.matmul(pSm[:, 0:448], lhsT=hT[:, k, ts], rhs=Wsmall[:, k, :], start=(k == 0), stop=(k == 7))
                return ins
            A("pe", smallp, ["hT"] + WSMALL_KEYS, [kSm], cost=1800)
            qi = sub % 2
            rms_rstd(pSm[:, 0:256], [kSm], 256, 1, hb[hi][:, 0:256], f"hb{hi}")
            A("dve", lambda e, qi=qi, pSm=pSm: e.tensor_scalar(out=qcn[qi][:], in0=pSm[:, 0:256], scalar1=st[:, 1:2],
                                                      scalar2=None, op0=ALU.mult),
              [kSm, "st1"], [f"qcn{qi}"], n=256)
            rms_rstd(pSm[:, 256:384], [kSm], 128, 2, hb[hi][:, 256:384], f"hb{hi}")
            A("dve", lambda e, qi=qi, pSm=pSm: e.scalar_tensor_tensor(out=ckvsb[qi][:], in0=pSm[:, 256:384], scalar=st[:, 2:3],
                                                             in1=gkvab[:], op0=ALU.mult, op1=ALU.mult),
              [kSm, "st2", "gkvab"], [f"ckvsb{qi}"], n=128)
            dma(lambda e, qi=qi, sub=sub: e.dma_start(out=ckv_dst(sub), in_=ckvsb[qi][:]), [f"ckvsb{qi}"], [], f"ckvst{qi}")
            ti = tm_idx(sub)
            A("dve", lambda e, qi=qi, ti=ti, pSm=pSm: e.tensor_tensor(out=rt[0][:], in0=pSm[:, 384:416], in1=costm[:, ti, :], op=ALU.mult),
              [kSm, "costm"], ["rt0"], n=32)
            A("dve", lambda e, qi=qi, ti=ti, pSm=pSm: e.tensor_tensor(out=rt[1][:], in0=pSm[:, 416:448], in1=sintm[:, ti, :], op=ALU.mult),
              [kSm, "sintm"], ["rt1"], n=32)
            A("pool", lambda e, qi=qi, pSm=pSm: e.tensor_tensor(out=krt[qi][:, 64:96], in0=rt[0][:], in1=rt[1][:], op=ALU.add),
              ["rt0", "rt1"], [f"krt{qi}"], n=32)
            dma(lambda e, qi=qi, sub=sub: e.dma_start(out=kr_dst(sub), in_=krt[qi][:, 64:96]), [f"krt{qi}"], [], f"krst{qi}")

            def trq(e, qi=qi):
                e.transpose(out=psT[:, 0:128], in_=qcn[qi][:, 0:128], identity=identb[:])
                return e.transpose(out=psT[:, 128:256], in_=qcn[qi][:, 128:256], identity=identb[:])
            A("pe", trq, [f"qcn{qi}", "identb"], ["psT"], cost=200)
            A("dve", lambda e, ts=ts: e.tensor_copy(out=qcT[:, :, ts], in_=psT[:, 0:256].rearrange("p (k n) -> p k n", k=2)),
              ["psT"], ["qcT"], n=256)
            pB, kB = nB()

            def trk(e, qi=qi, pB=pB):
                e.transpose(out=pB[:, 0:128], in_=ckvsb[qi][:], identity=identf[:])
                return e.transpose(out=pB[0:96, 128:256], in_=krt[qi][:], identity=identf[:])
            A("pe", trk, [f"ckvsb{qi}", f"krt{qi}", "identf"], [kB], cost=300)
            A("dve", lambda e, ts=ts, pB=pB: e.tensor_copy(out=ckvT[:, ts], in_=pB[:, 0:128]), [kB], ["ckvT"], n=128)
            A("dve", lambda e, ts=ts, pB=pB: e.tensor_copy(out=krT[64:96, ts], in_=pB[64:96, 128:256]), [kB], ["krT"], n=128)

    def big_chunk(cc, ntok):
        wi = nxt("Wb", NWB)
        dma(lambda e, wi=wi, cc=cc: e.dma_start(out=Wb[wi][:].rearrange("p k n -> p (k n)"), in_=wbigD[cc, :, :]),
            [f"wbigD{cc // 8}_{k_}" for k_ in range(8)], [f"Wb{wi}"], f"Wb{wi}")
        pB, kB = nB()

        def mm(e, wi=wi, pB=pB):
            ins = None
            for k in range(8):
                ins = e.matmul(pB[:, 0:ntok], lhsT=Wb[wi][:, k, :], rhs=hT[:, k, 0:ntok], start=(k == 0), stop=(k == 7))
            return ins
        A("pe", mm, [f"Wb{wi}", "hT"], [kB], cost=8 * (ntok * 0.5 + 20))
        return pB, kB

    def stage_C1(ntok, glu_dst, glu32_dst):
        for c in range(4):
            pBb, kBb = big_chunk(4 + c, ntok)
            th, kth = nfp()
            A("act", lambda e, pBb=pBb, th=th: e.activation(out=th[:, 0:ntok], in_=pBb[:, 0:ntok], func=AF.Tanh),
              [kBb], [kth], n=ntok)
            pBa, kBa = big_chunk(c, ntok)
            for (dst, dkeys) in glu_dst(c):
                A("dve", lambda e, pBa=pBa, th=th, dst=dst: e.scalar_tensor_tensor(
                    out=dst[1], in0=th[:, dst[0]], scalar=1.0, in1=pBa[:, dst[0]], op0=ALU.add, op1=ALU.mult),
                  [kBa, kth], dkeys, n=ntok // len(glu_dst(c)))
            if glu32_dst is not None:
                d32 = glu32_dst(c)
                A("dve", lambda e, pBa=pBa, th=th, d32=d32: e.scalar_tensor_tensor(
                    out=d32[1], in0=th[:, d32[0]], scalar=1.0, in1=pBa[:, d32[0]], op0=ALU.add, op1=ALU.mult),
                  [kBa, kth], ["glu32"], n=128)
        gates(8, ntok)

    def gates(cc0, ntok, dst=None, dkey="sg"):
        dst = sg if dst is None else dst
        for c in range(4):
            pBg, kBg = big_chunk(cc0 + c, ntok)
            th, kth = nfp()
            A("act", lambda e, pBg=pBg, th=th: e.activation(out=th[:, 0:ntok], in_=pBg[:, 0:ntok], func=AF.Tanh),
              [kBg], [kth], n=ntok)
            A("dve", lambda e, pBg=pBg, th=th, c=c: e.scalar_tensor_tensor(
                out=dst[:, c, 0:ntok], in0=th[:, 0:ntok], scalar=1.0, in1=pBg[:, 0:ntok], op0=ALU.add, op1=ALU.mult),
              [kBg, kth], [dkey], n=ntok)

    def stage_D(ntok, conv_mm, g_dst, g_src, conv_cost, g_store, gw, nparts):
        pM, kM = nA()
        pQ, kQ = psB[0], "psB0"
        for c in range(4):
            gi = nxt("Dc", NGT)
            dma(lambda e, c=c: e.dma_start(out=gluD[c, :, 0:gw], in_=g_store(c)), [f"glu{c}"], [f"gluD{c}"], f"gluDst{c}", cost=3000)
            for sh in range(4):
                for part in range(nparts):
                    dma(lambda e, gi=gi, c=c, sh=sh, part=part: e.dma_start(
                        out=g_dst(Gt[gi], sh, part), in_=g_src(gluD[c, :, 0:gw].rearrange("(g q) t -> q g t", g=4), sh, part)),
                        [f"gluD{c}"], [f"Gt{gi}_{sh}_{part}"], f"Gt{gi}_{sh}_{part}", cost=3500)
            pB, kB = psB[1], "psB1"
            A("pe", lambda e, gi=gi, c=c, pB=pB: conv_mm(e, Gt[gi], c, pB),
              [f"Gt{gi}_{s_}_{p_}" for s_ in range(4) for p_ in range(nparts)] + [f"Wp{c}"], [kB], cost=conv_cost)
            yi = nxt("ysq", 2)
            A("act", lambda e, pB=pB, c=c: e.activation(out=ybf[:, c, 0:ntok], in_=pB[:, 0:ntok], func=AF.Identity,
                                                         bias=vec[:, 10 + c:11 + c]),
              [kB, "vec"], [f"ybf{c}"], n=ntok)
            A("act", lambda e, pB=pB, c=c, yi=yi: e.activation(out=ysq[yi][:, 0:ntok], in_=pB[:, 0:ntok], func=AF.Square,
                                                                bias=vec[:, 10 + c:11 + c]),
              [kB, "vec"], [f"ysq{yi}"], n=ntok)
            A("pe", lambda e, c=c: e.matmul(pM[:, 0:ntok], lhsT=onesS[:], rhs=ybf[:, c, 0:ntok], start=(c == 0), stop=(c == 3)),
              [f"ybf{c}", "onesS"], [kM], cost=ntok * 0.45 + 20)
            A("pe", lambda e, c=c, yi=yi: e.matmul(pQ[:, 0:ntok], lhsT=onesS[:], rhs=ysq[yi][:, 0:ntok], start=(c == 0), stop=(c == 3)),
              [f"ysq{yi}", "onesS"], [kQ], cost=ntok * 0.45 + 20)
        m2, km2 = nfp()
        A("act", lambda e: e.activation(out=m2[:, 0:ntok], in_=pM[:, 0:ntok], func=AF.Square), [kM], [km2], n=ntok)
        A("dve", lambda e: e.scalar_tensor_tensor(out=rstdln[:, 0:ntok], in0=pQ[:, 0:ntok], scalar=EPS, in1=m2[:, 0:ntok],
                                                  op0=ALU.add, op1=ALU.subtract),
          [kQ, km2], ["rstdln"], n=ntok)
        A("act", lambda e: e.activation(out=rstdln[:, 0:ntok], in_=rstdln[:, 0:ntok], func=AF.Ln), ["rstdln"], ["rstdln"], cost=1600 + ntok)
        A("act", lambda e: e.activation(out=rstdln[:, 0:ntok], in_=rstdln[:, 0:ntok], func=AF.Exp, scale=-0.5), ["rstdln"], ["rstdln"], n=ntok)
        for c in range(4):
            t1, k1 = nfp()
            A("dve", lambda e, c=c, t1=t1: e.tensor_tensor(out=t1[:, 0:ntok], in0=ybf[:, c, 0:ntok], in1=pM[:, 0:ntok], op=ALU.subtract),
              [f"ybf{c}", kM], [k1], n=ntok)
            A("pool", lambda e, t1=t1: e.tensor_tensor(out=t1[:, 0:ntok], in0=t1[:, 0:ntok], in1=rstdln[:, 0:ntok], op=ALU.mult),
              [k1, "rstdln"], [k1], n=ntok)
            th, kth = nfp()
            A("act", lambda e, c=c, t1=t1, th=th: e.activation(out=th[:, 0:ntok], in_=t1[:, 0:ntok], func=AF.Tanh,
                                                                scale=vech[:, 14 + c:15 + c], bias=vech[:, 18 + c:19 + c]),
              [k1, "vech"], [kth], cost=1600 + ntok)
            zz, kzz = nfp()
            A("dve", lambda e, c=c, t1=t1, zz=zz: e.tensor_scalar(out=zz[:, 0:ntok], in0=t1[:, 0:ntok], scalar1=vech[:, 14 + c:15 + c],
                                                                  scalar2=vech[:, 18 + c:19 + c], op0=ALU.mult, op1=ALU.add),
              [k1, "vech"], [kzz], n=ntok)
            A("dve", lambda e, th=th, zz=zz: e.scalar_tensor_tensor(out=zz[:, 0:ntok], in0=th[:, 0:ntok], scalar=1.0, in1=zz[:, 0:ntok],
                                                                    op0=ALU.add, op1=ALU.mult),
              [kth, kzz], [kzz], n=ntok)
            A("pool", lambda e, c=c, zz=zz: e.tensor_tensor(out=mixedT[:, c, 0:ntok], in0=zz[:, 0:ntok], in1=sg[:, c, 0:ntok], op=ALU.mult),
              [kzz, "sg"], [f"mixedT{c}"], n=ntok)

    def stage_E_q(ntok, rope_col0):
        dma(lambda e: e.dma_start(out=cosT[64:96, 0:ntok], in_=cosT_d[:, rope_col0:rope_col0 + ntok]), [], ["cosT"], "cosT")
        dma(lambda e: e.dma_start(out=sinT[64:96, 0:ntok], in_=sinT_d[:, rope_col0:rope_col0 + ntok]), [], ["sinT"], "sinT")
        for h in range(NH):
            pa, ka = nB()
            pb, kb = nB()

            def mmq(e, h=h, pa=pa, pb=pb):
                ins = None
                for v, pp in ((0, pa), (1, pb)):
                    for k2 in range(2):
                        off = (h * 2 + v) * 96
                        ins = e.matmul(pp[0:96, 0:ntok], lhsT=Wqb[:, k2, off:off + 96], rhs=qcT[:, k2, 0:ntok],
                                       start=(k2 == 0), stop=(k2 == 1))
                return ins
            A("pe", mmq, WQB_KEYS + ["qcT"], [ka, kb], cost=4 * (ntok * 0.45 + 20))
            A("act", lambda e, h=h, pa=pa: e.activation(out=QT[0:64, h, 0:ntok], in_=pa[0:64, 0:ntok], func=AF.Copy),
              [ka], [f"QT{h}n"], n=ntok)
            r1, k1 = nfp()
            r2, k2_ = nfp()
            A("dve", lambda e, pa=pa, r1=r1: e.tensor_tensor(out=r1[64:96, 0:ntok], in0=pa[64:96, 0:ntok], in1=cosT[64:96, 0:ntok], op=ALU.mult),
              [ka, "cosT"], [k1], n=ntok)
            A("dve", lambda e, pb=pb, r2=r2: e.tensor_tensor(out=r2[64:96, 0:ntok], in0=pb[64:96, 0:ntok], in1=sinT[64:96, 0:ntok], op=ALU.mult),
              [kb, "sinT"], [k2_], n=ntok)
            A("pool", lambda e, h=h, r1=r1, r2=r2: e.tensor_tensor(out=QT[64:96, h, 0:ntok], in0=r1[64:96, 0:ntok], in1=r2[64:96, 0:ntok], op=ALU.add),
              [k1, k2_], [f"QT{h}r"], n=ntok)

    def ktk(h, c0, c1):
        ks = []
        if c0 < 1056:
            ks.append(f"KT{h}a")
        if c1 > 1056:
            ks.append(f"KT{h}b")
        return ks

    def kv_project(ntok, key0, src_T, src_key, kr_src, kr_key, vb0=None, bcast=False, bsem="KTrb"):
        vb0 = key0 // 128 if vb0 is None else vb0
        for hp in range(4):
            pa, ka = (nB() if KV_PSB else nA())
            A("pe", lambda e, hp=hp, pa=pa: e.matmul(pa[:, 0:ntok], lhsT=Wkv[:, hp * 128:(hp + 1) * 128], rhs=src_T[:, 0:ntok],
                                                     start=True, stop=True),
              ["Wkv", src_key], [ka], cost=ntok * 0.45 + 20)
            A("dve", lambda e, hp=hp, pa=pa: e.tensor_copy(out=KT[0:64, 2 * hp, key0:key0 + ntok], in_=pa[0:64, 0:ntok]),
              [ka], ktk(2 * hp, key0, key0 + ntok), n=ntok)
            A("dve", lambda e, hp=hp, pa=pa: e.tensor_copy(out=KT[0:64, 2 * hp + 1, key0:key0 + ntok], in_=pa[64:128, 0:ntok]),
              [ka], ktk(2 * hp + 1, key0, key0 + ntok), n=ntok)
        if bcast:
            dma(lambda e: e.dma_start(out=KT[64:96, :, key0:key0 + ntok],
                                      in_=kr_src[64:96, 0:ntok].unsqueeze(1).broadcast_to([32, NH, ntok])),
                [kr_key], [k_ for h in range(NH) for k_ in ktk(h, key0, key0 + ntok)], bsem)
        else:
            for h in range(NH):
                dma(lambda e, h=h: e.dma_start(out=KT[64:96, h, key0:key0 + ntok], in_=kr_src[64:96, 0:ntok]),
                    [kr_key], ktk(h, key0, key0 + ntok), f"KTr{h}")
        nblk = (ntok + 127) // 128
        for b in range(nblk):
            n = min(128, ntok - b * 128)
            kb = vb0 + b
            pa, ka = (nB() if KV_PSB else nA())
            A("pe", lambda e, b=b, n=n, pa=pa: e.matmul(pa[0:n, 0:512], lhsT=src_T[:, b * 128:b * 128 + n], rhs=Wkv[:, 512:1024],
                                                        start=True, stop=True),
              ["Wkv", src_key], [ka], cost=260)
            A("dve", lambda e, kb=kb, n=n, pa=pa: e.tensor_copy(out=V[0:n, kb, :], in_=pa[0:n, 0:512]), [ka], [f"V{kb}"], n=512)

    def attention(nq, q_lo, blocks, sg_rows, gsz=1):
        groups = [blocks[i:i + gsz] for i in range(0, len(blocks), gsz)]
        if gsz > 1 and len(groups[-1]) > 1 and groups[-1][-1][2] != 128:
            last = groups[-1].pop()
            groups.append([last])
        nb = len(blocks)
        for h in range(NH):
            hp = h % 2
            oi = h % 2
            pO = psO[:, oi * 512:(oi + 1) * 512]
            kO = f"psO{oi}"
            bi0 = 0
            for grp in groups:
                q0 = grp[0][3]
                diag = grp[0][4]
                n = nq - q0
                ng = len(grp)
                nkm = max(b_[2] for b_ in grp)
                pa, ka = nA()

                def smm(e, h=h, grp=grp, q0=q0, n=n, pa=pa):
                    ins = None
                    for i, (kc0, kb, nk, _q, _d) in enumerate(grp):
                        ins = e.matmul(pa[0:nk, i * n:(i + 1) * n], lhsT=KT[0:96, h, kc0:kc0 + nk], rhs=QT[0:96, h, q_lo + q0:q_lo + nq],
                                       start=True, stop=True)
                    return ins
                kreads = []
                for (kc0, kb, nk, _q, _d) in grp:
                    for k_ in ktk(h, kc0, kc0 + nk):
                        if k_ not in kreads:
                            kreads.append(k_)
                A("pe", smm, kreads + [f"QT{h}n", f"QT{h}r"], [ka], cost=ng * max(n * 0.66 + 30, 70))
                pi = nxt("PT", NPT)
                Pt, kP = PT[pi], f"PT{pi}"
                A("act", lambda e, nkm=nkm, n=n, ng=ng, pa=pa, Pt=Pt: e.activation(out=Pt[0:nkm, 0:ng * n], in_=pa[0:nkm, 0:ng * n], func=AF.Exp,
                                                                                   scale=ATT_SCALE),
                  [ka], [kP], n=ng * n)
                if diag:
                    A("pool", lambda e, Pt=Pt: e.memset(Pt[64:128, 0:64], 0.0), [], [kP], cost=200)

                def pv(e, h=h, grp=grp, q0=q0, n=n, Pt=Pt, bi0=bi0, pO=pO):
                    ins = None
                    for i, (kc0, kb, nk, _q, _d) in enumerate(grp):
                        bi = bi0 + i
                        e.matmul(pO[0:64, q0:nq], lhsT=V[0:nk, kb, h * 64:(h + 1) * 64], rhs=Pt[0:nk, i * n:(i + 1) * n],
                                 start=(bi == 0), stop=(bi == nb - 1))
                        ins = e.matmul(pO[64:128, q0:nq], lhsT=vones[0:nk, :], rhs=Pt[0:nk, i * n:(i + 1) * n],
                                       start=(bi == 0), stop=(bi == nb - 1))
                    return ins
                A("pe", pv, [f"V{b_[1]}" for b_ in grp] + ["vones", kP], [kO], cost=ng * max(n * 0.75 + 40, 100))
                bi0 += ng
            lo, hi_ = hp * 64, hp * 64 + 64
            krl, ko1 = f"rl{hp}", f"o1{hp}"
            A("dve", lambda e, pO=pO, lo=lo, hi_=hi_: e.reciprocal(out=rlb[lo:hi_, 0:nq], in_=pO[64:128, 0:nq]), [kO], [krl],
              cost=120 + 4.4 * nq)
            A("dve", lambda e, pO=pO, lo=lo, hi_=hi_: e.tensor_tensor(out=o1b[lo:hi_, 0:nq], in0=pO[0:64, 0:nq], in1=rlb[lo:hi_, 0:nq],
                                                                       op=ALU.mult),
              [kO, krl], [ko1], n=nq)
            A("pool", lambda e, h=h, lo=lo, hi_=hi_: e.tensor_tensor(
                out=mixedT[lo:hi_, 4 + h // 2, q_lo:q_lo + nq], in0=o1b[lo:hi_, 0:nq], in1=sga[lo:hi_, h // 2, sg_rows], op=ALU.mult),
              [ko1, "sga"], [f"mixedT{4 + h // 2}"], n=nq)

    gbanks = [(psA[0], "psA0"), (psA[1], "psA1"), (psO[:, 0:512], "psO0"), (psO[:, 512:1024], "psO1")]

    def stage_G(nsub, x_src, y_dst):
        for sub in range(nsub):
            ts = slice(sub * 128, (sub + 1) * 128)
            xi = nxt("xr", 2)
            dma(lambda e, xi=xi, sub=sub: e.dma_start(out=xres[xi][:], in_=x_src(sub)), [], [f"xres{xi}"], f"xres{xi}")
            g0 = nxt("G", 2) * 2
            bks = [gbanks[g0], gbanks[g0 + 1]]
            for half in range(2):
                pb, kb = bks[half]

                def mmo(e, ts=ts, half=half, pb=pb):
                    ins = None
                    for k in range(8):
                        ins = e.matmul(pb[:, 0:512], lhsT=mixedT[:, k, ts], rhs=Wout[:, k, half * 512:(half + 1) * 512],
                                       start=(k == 0), stop=(k == 7))
                    return ins
                A("pe", mmo, [f"mixedT{k}" for k in range(8)] + WOUT_KEYS, [kb], cost=8 * 250)
                A("act", lambda e, half=half, pb=pb: e.activation(out=junk[:, half * 512:(half + 1) * 512], in_=pb[:, 0:512], func=AF.Square,
                                                                  scale=1.0 / 32.0, accum_out=st[:, 3 + half:4 + half]),
                  [kb], ["junk", f"st{3 + half}"], n=512)
            A("pool", lambda e: e.tensor_tensor(out=st[:, 3:4], in0=st[:, 3:4], in1=st[:, 4:5], op=ALU.add), ["st3", "st4"], ["st3"], cost=300)
            A("pool", lambda e: e.tensor_scalar(out=st[:, 3:4], in0=st[:, 3:4], scalar1=EPS, scalar2=None, op0=ALU.add),
              ["st3"], ["st3"], cost=300)
            A("pool", lambda e: e.tensor_tensor(out=st[:, 3:4], in0=st[:, 3:4], in1=neghalf[:, 0:1], op=ALU.pow),
              ["st3", "neghalf"], ["st3"], cost=550)
            for half in range(2):
                pb, kb = bks[half]
                hs = slice(half * 512, (half + 1) * 512)
                A("dve", lambda e, pb=pb, hs=hs: e.scalar_tensor_tensor(out=tmpo[:, hs], in0=pb[:, 0:512], scalar=st[:, 3:4], in1=gpostb[:, hs],
                                                                        op0=ALU.mult, op1=ALU.mult),
                  [kb, "st3", "gpostb"], [f"tmpo{half}"], n=512)
            A("pool", lambda e, xi=xi: e.tensor_tensor(out=xres[xi][:], in0=tmpo[:], in1=xres[xi][:], op=ALU.add),
              ["tmpo0", "tmpo1", f"xres{xi}"], [f"xres{xi}"], n=1024)
            dma(lambda e, xi=xi, sub=sub: e.dma_start(out=y_dst(sub), in_=xres[xi][:]), [f"xres{xi}"], [], f"xresst{xi}")

    def sample_phase():
        NS = 128
        stage_AB(lambda sub: xs[:, :], 1, lambda sub: 16, lambda sub: ockv_s[:, :], lambda sub: okr_s[:, :])
        for s in range(NSEQ_S):
            dma(lambda e, s=s: e.dma_start(out=osb[0:30, :], in_=sconv[s, :, :]), [], ["osb"], "hist")
            pB, kB = nB()

            def trh(e, pB=pB):
                ins = None
                for c in range(4):
                    ins = e.transpose(out=pB[:, c * 32:c * 32 + 30], in_=osb[0:30, c * 128:(c + 1) * 128], identity=identf[0:30, 0:30])
                return ins
            A("pe", trh, ["osb", "identf"], [kB])
            A("dve", lambda e, s=s, pB=pB: e.tensor_copy(out=gluS[:, :, s, 0:30],
                                                         in_=pB[:, 0:128].rearrange("p (c n) -> p c n", c=4)[:, :, 0:30]),
              [kB], GLUK)
        stage_C1(NS,
                 lambda c: [((slice(s * 32, (s + 1) * 32), gluS[:, c, s, 30:62]), [f"glu{c}"]) for s in range(NSEQ_S)],
                 lambda c: (slice(0, NS), glu32[:, c, :]))

        def conv_mm_s(e, G, c, pB):
            ins = None
            for q in range(NSEQ_S):
                for r in range(8):
                    for g in range(4):
                        ins = e.matmul(pB[g * 32:(g + 1) * 32, q * 32:(q + 1) * 32], lhsT=Wp[:, c, g, r, :],
                                       rhs=G[:, g, q * 39 + r:q * 39 + r + 32], start=(r == 0), stop=(r == 7),
                                       tile_position=(0, g * 32))
            return ins
        stage_D(NS, conv_mm_s,
                lambda G, sh, part: G[sh * 32:(sh + 1) * 32, :, part * 39:(part + 1) * 39],
                lambda D, sh, part: D[:, :, part * 64 + 8 * sh:part * 64 + 8 * sh + 39],
                NSEQ_S * 8 * 80,
                lambda c: gluS[:, c, :, :].rearrange("p s t -> p (s t)"), NSEQ_S * 64, NSEQ_S)
        pB, kB = nB()

        def trgs(e, pB=pB):
            ins = None
            for c in range(4):
                ins = e.transpose(out=pB[:, c * 128:(c + 1) * 128], in_=glu32[:, c, :], identity=identf[:])
            return ins
        A("pe", trgs, ["glu32", "identf"], [kB])
        A("dve", lambda e, pB=pB: e.tensor_copy(out=osb[:], in_=pB[:, :]), [kB], ["osb"])
        for s in range(NSEQ_S):
            dma(lambda e, s=s: e.dma_start(out=oconv_s[s, :, :], in_=osb[s * 32 + 2:s * 32 + 32, :]), ["osb"], [], "osbst")
        gates(12, NS, sga, "sga")
        stage_E_q(NS, SEQ)
        for s in range(NSEQ_S):
            cbase, vbase = (s % 2) * 1056, (s % 2) * 9
            for half in range(2):
                fc, kfc = nfp()
                fr, kfr = nfp()
                dma(lambda e, s=s, half=half, fc=fc: e.dma_start(out=fc[:, 0:512].rearrange("p (b f) -> p b f", b=4),
                                                                in_=cckv[s, half * 512:(half + 1) * 512, :].rearrange("(b p) f -> p b f", p=128)),
                    [], [kfc], f"ld{kfc}")
                dma(lambda e, s=s, half=half, fr=fr: e.dma_start(out=fr[:, 0:384].rearrange("p (b f) -> p b f", b=4)[:, :, 64:96],
                                                                in_=ckr[s, half * 512:(half + 1) * 512, :].rearrange("(b p) f -> p b f", p=128)),
                    [], [kfr], f"ld{kfr}")
                for b4 in range(4):
                    pB, kB = nB()

                    def trk2(e, b4=b4, pB=pB, fc=fc, fr=fr):
                        e.transpose(out=pB[:, 0:128], in_=fc[:, b4 * 128:(b4 + 1) * 128], identity=identf[:])
                        return e.transpose(out=pB[0:96, 128:256], in_=fr[:, b4 * 96:(b4 + 1) * 96], identity=identf[:])
                    A("pe", trk2, [kfc, kfr, "identf"], [kB], cost=300)
                    ts = slice(b4 * 128, (b4 + 1) * 128)
                    A("dve", lambda e, ts=ts, pB=pB, half=half: e.tensor_copy(out=hT[:, half, ts], in_=pB[:, 0:128]), [kB], ["hT", f"cT{half}"], n=128)
                    A("dve", lambda e, ts=ts, pB=pB, half=half: e.tensor_copy(out=hT[64:96, 2 + half, ts], in_=pB[64:96, 128:256]), [kB], ["hT", f"cR{half}"], n=128)
                kv_project(512, cbase + half * 512, hT[:, half, :], f"cT{half}", hT[:, 2 + half, :], f"cR{half}", vb0=vbase + half * 4, bcast=True, bsem=f"KTrb{s % 2}{half}")
            kv_project(32, cbase + 1024, ckvT[:, s * 32:(s + 1) * 32], "ckvT", krT[:, s * 32:(s + 1) * 32], "krT", vb0=vbase + 8, bcast=True, bsem=f"KTrb{s % 2}n")
            blocks = [(cbase + kb * 128, vbase + kb, 128, 0, False) for kb in range(8)] + [(cbase + 1024, vbase + 8, 32, 0, False)]
            import os as _os
            if _os.environ.get("DBG_BLOCKS") == "cached":
                blocks = blocks[:8]
            elif _os.environ.get("DBG_BLOCKS") == "new":
                blocks = blocks[8:]
            elif _os.environ.get("DBG_BLOCKS") == "first":
                blocks = blocks[:1]
            attention(32, s * 32, blocks, slice(s * 32, (s + 1) * 32), gsz=4)
        if debug:
            dbg = nc.dram_tensor("dbg", [128, 8, 512], BF16, kind="ExternalOutput").ap()
            import os as _os
            if _os.environ.get("DBG_DUMP") == "hT":
                dma(lambda e: e.dma_start(out=dbg[:, :, :], in_=hT[:]), ["hT", "cT0", "cT1", "cR0", "cR1"], [], "dbg")
            elif _os.environ.get("DBG_DUMP") == "KT":
                dma(lambda e: e.dma_start(out=dbg[:, :, :], in_=KT[:, :, 0:512]), [f"KT{h}" for h in range(8)], [], "dbg")
            elif _os.environ.get("DBG_DUMP") == "V":
                dma(lambda e: e.dma_start(out=dbg[:, :, :], in_=V[:, 0:8, :]), [f"V{h}" for h in range(8)], [], "dbg")
            else:
                dma(lambda e: e.dma_start(out=dbg[:, :, :], in_=mixedT[:]), [f"mixedT{k}" for k in range(8)], [], "dbg")
        stage_G(1, lambda sub: xs[:, :], lambda sub: ys[:, :])


    def ab_prompt(p, t):
        t0 = t * TILE
        stage_AB(lambda sub, p=p, t0=t0: xp[p, t0 + sub * 128:t0 + (sub + 1) * 128, :], 4,
                 lambda sub, t=t: 4 * t + sub,
                 lambda sub, p=p, t0=t0: ockv_p[p, t0 + sub * 128:t0 + (sub + 1) * 128, :],
                 lambda sub, p=p, t0=t0: okr_p[p, t0 + sub * 128:t0 + (sub + 1) * 128, :])

    if do_prompt:
        ab_prompt(0, 0)
    setup_rest()
    if SAMPLE_POS == 'start':
        sample_phase()
    for p in range(NSEQ_P if do_prompt else 0):
        if p == 1 and SAMPLE_POS == 'mid':
            sample_phase()
        A("pool", lambda e: e.memset(glu[:, :, 0:30], 0.0), [], GLUK)
        for t in range(SEQ // TILE):
            t0 = t * TILE
            if (p, t) != (0, 0):
                ab_prompt(p, t)
            _unused = (lambda sub, p=p, t0=t0: xp[p, t0 + sub * 128:t0 + (sub + 1) * 128, :], 4,
                     lambda sub, t=t: 4 * t + sub,
                     lambda sub, p=p, t0=t0: ockv_p[p, t0 + sub * 128:t0 + (sub + 1) * 128, :],
                     lambda sub, p=p, t0=t0: okr_p[p, t0 + sub * 128:t0 + (sub + 1) * 128, :])
            last = (t == SEQ // TILE - 1)
            stage_C1(TILE,
                     lambda c: [((slice(0, TILE), glu[:, c, 30:30 + TILE]), [f"glu{c}"])],
                     (lambda c: (slice(TILE - 128, TILE), glu32[:, c, :])) if last else None)

            def conv_mm(e, G, c, pB):
                ins = None
                for r in range(8):
                    for g in range(4):
                        ins = e.matmul(pB[g * 32:(g + 1) * 32, 0:TILE], lhsT=Wp[:, c, g, r, :], rhs=G[:, g, r:r + TILE],
                                       start=(r == 0), stop=(r == 7), tile_position=(0, g * 32))
                return ins
            stage_D(TILE, conv_mm,
                    lambda G, sh, part: G[sh * 32:(sh + 1) * 32, :, 0:519],
                    lambda D, sh, part: D[:, :, 8 * sh:8 * sh + 519],
                    2750,
                    lambda c: glu[:, c, :], 544, 1)
            A("pool", lambda e: e.tensor_copy(out=glu[:, :, 0:30], in_=glu[:, :, TILE:TILE + 30]), GLUK, GLUK)
            if last:
                pB, kB = nB()

                def trg(e, pB=pB):
                    ins = None
                    for c in range(4):
                        ins = e.transpose(out=pB[:, c * 128:(c + 1) * 128], in_=glu32[:, c, :], identity=identf[:])
                    return ins
                A("pe", trg, ["glu32", "identf"], [kB])
                A("dve", lambda e, pB=pB: e.tensor_copy(out=osb[:], in_=pB[:, :]), [kB], ["osb"])
                dma(lambda e, p=p: e.dma_start(out=oconv_p[p, :, :], in_=osb[98:128, :]), ["osb"], [], "osbst")
            gates(12, TILE, sga, "sga")
            stage_E_q(TILE, t0)
            kv_project(TILE, t0, ckvT, "ckvT", krT, "krT")
            blocks = []
            for kb in range(4 * t + 4):
                j = kb - 4 * t
                if j < 0:
                    blocks.append((kb * 128, kb, 128, 0, False))
                else:
                    blocks.append((kb * 128, kb, 128, 128 * j, True))
            attention(TILE, 0, blocks, slice(0, TILE))
            stage_G(4, lambda sub, p=p, t0=t0: xp[p, t0 + sub * 128:t0 + (sub + 1) * 128, :],
                    lambda sub, p=p, t0=t0: yp[p, t0 + sub * 128:t0 + (sub + 1) * 128, :])

    if SAMPLE_POS == 'end':
        sample_phase()

    S.resolve()
    sems = {}
    for en in ("pe", "act", "dve", "pool"):
        sems[en] = es.enter_context(nc.semaphore(f"sem_{en}"))
    dma_sems = {}
    for k in S.dma_count:
        dma_sems[k] = es.enter_context(nc.semaphore(f"sd_{k}"))
    with es:
        with nc.Block() as block:
            @block.sync
            def _(e):
                S.run_engine("sp", e, sems, dma_sems, final_wait=True)

            @block.tensor
            def _(e):
                S.run_engine("pe", e, sems, dma_sems)

            @block.scalar
            def _(e):
                S.run_engine("act", e, sems, dma_sems)

            @block.vector
            def _(e):
                S.run_engine("dve", e, sems, dma_sems)

            @block.gpsimd
            def _(e):
                S.run_engine("pool", e, sems, dma_sems)
    return nc


def _rope_tables():
    inv = (10000.0 ** (-np.arange(0, 32, 2, dtype=np.float32) / 32.0)).astype(np.float32)
    pos = np.arange(SEQ, dtype=np.float32)
    ang = (pos[:, None] * inv[None, :]).astype(np.float32)
    cos = np.cos(ang).astype(np.float32)
    sin = np.sin(ang).astype(np.float32)
    cos_full = np.concatenate([cos, cos], axis=1)
    sin_sgn = np.concatenate([-sin, sin], axis=1)
    spos = PAST + (np.arange(128) % DEC_SEQ)
    costm = np.zeros((128, 17, 32), np.float32)
    sintm = np.zeros((128, 17, 32), np.float32)
    for i in range(16):
        costm[:, i, :] = cos_full[i * 128:(i + 1) * 128]
        sintm[:, i, :] = sin_sgn[i * 128:(i + 1) * 128]
    costm[:, 16, :] = cos_full[spos]
    sintm[:, 16, :] = sin_sgn[spos]
    cosT = np.concatenate([cos_full.T, cos_full[spos].T], axis=1)
    sinT = np.concatenate([sin_sgn.T, sin_sgn[spos].T], axis=1)
    return (np.ascontiguousarray(costm), np.ascontiguousarray(sintm),
            np.ascontiguousarray(cosT), np.ascontiguousarray(sinT))


_NC_CACHE = {}


def kernel(x_prompt, x_sample, cache_ckv, cache_krope, state_conv, g_pre, w_in, conv_w, conv_b,
           conv_ln_g, conv_ln_b, g_qa, w_qb, g_kva, w_kvb, w_out, g_post):
    f = lambda a: np.asarray(a, dtype=np.float32)
    x_prompt, x_sample = f(x_prompt), f(x_sample)
    cache_ckv, cache_krope, state_conv = f(cache_ckv)[0], f(cache_krope)[0], f(state_conv)[0]
    g_pre, w_in, conv_w, conv_b = f(g_pre)[0], f(w_in)[0], f(conv_w)[0], f(conv_b)[0]
    conv_ln_g, conv_ln_b, g_qa, w_qb = f(conv_ln_g)[0], f(conv_ln_b)[0], f(g_qa)[0], f(w_qb)[0]
    g_kva, w_kvb, w_out, g_post = f(g_kva)[0], f(w_kvb)[0], f(w_out)[0], f(g_post)[0]

    def pk(w, kc):
        return np.ascontiguousarray(w.reshape(kc, 128, w.shape[1]).transpose(1, 0, 2))

    w_a, w_b, w_gc = w_in[:, 0:512], w_in[:, 512:1024], w_in[:, 1024:1536]
    w_q, w_kvc, w_kr, w_ga = w_in[:, 1536:1792], w_in[:, 1792:1920], w_in[:, 1920:1952], w_in[:, 1952:2464]
    rot = np.concatenate([np.arange(16, 32), np.arange(0, 16)])
    wbig = pk(np.concatenate([w_a, w_b, w_gc, w_ga], axis=1), 8)
    wsmall = pk(np.concatenate([w_q, w_kvc, w_kr, w_kr[:, rot]], axis=1), 8)
    qh = w_qb.reshape(256, NH, 96)
    q0 = qh
    q1 = np.concatenate([qh[:, :, 0:64], qh[:, :, 64:96][:, :, rot]], axis=2)
    wqb = pk(np.stack([q0, q1], axis=2).reshape(256, NH * 2 * 96), 2)
    kvh = w_kvb.reshape(128, NH, 128)
    wkv = np.ascontiguousarray(np.concatenate([kvh[:, :, 0:64].reshape(128, 512), kvh[:, :, 64:128].reshape(128, 512)], axis=1))
    wout = pk(w_out, 8)
    cwpad = np.concatenate([conv_w, np.zeros((1, 512), np.float32)], axis=0)
    convwp = np.ascontiguousarray(cwpad.reshape(4, 8, 4, 4, 32).transpose(0, 4, 2, 3, 1).reshape(128, 4, 4, 8))
    estack = np.ascontiguousarray(np.tile(np.eye(32, dtype=np.float32), (4, 1)))
    vecs = np.zeros((128, 32), np.float32)
    vecs[:, 0:8] = g_pre.reshape(8, 128).T
    vecs[:, 8:10] = g_qa.reshape(2, 128).T
    vecs[:, 10:14] = conv_b.reshape(4, 128).T
    vecs[:, 14:18] = conv_ln_g.reshape(4, 128).T
    vecs[:, 18:22] = conv_ln_b.reshape(4, 128).T
    gpost_b = np.ascontiguousarray(np.broadcast_to(g_post[None, :], (128, 1024)))
    gkva_b = np.ascontiguousarray(np.broadcast_to(g_kva[None, :], (128, 128)))
    ident = np.eye(128, dtype=np.float32)
    costm, sintm, cosT, sinT = _rope_tables()

    if "nc" not in _NC_CACHE:
        _NC_CACHE["nc"] = build_program()
    nc = _NC_CACHE["nc"]

    in_maps = []
    for c in range(NCORES):
        in_maps.append({
            "xp": np.ascontiguousarray(x_prompt[c * NSEQ_P:(c + 1) * NSEQ_P]),
            "xs": np.ascontiguousarray(x_sample[c * NSEQ_S:(c + 1) * NSEQ_S].reshape(128, D_MODEL)),
            "cckv": np.ascontiguousarray(cache_ckv[c * NSEQ_S:(c + 1) * NSEQ_S]),
            "ckr": np.ascontiguousarray(cache_krope[c * NSEQ_S:(c + 1) * NSEQ_S]),
            "sconv": np.ascontiguousarray(state_conv[c * NSEQ_S:(c + 1) * NSEQ_S]),
            "wbig": wbig, "wsmall": wsmall, "wqb": wqb, "wkv": wkv, "wout": wout, "convwp": convwp, "estack": estack,
            "vecs": vecs, "gpost_b": gpost_b, "gkva_b": gkva_b, "ident": ident,
            "costm": costm, "sintm": sintm, "cosT": cosT, "sinT": sinT,
        })
    res = run_bass_kernel_spmd(nc, in_maps, core_ids=list(range(NCORES)))
    R = res.results
    cat = lambda k: np.concatenate([np.asarray(r[k], dtype=np.float32) for r in R], axis=0)
    y_prompt = cat("yp")
    y_sample = cat("ys").reshape(32, DEC_SEQ, D_MODEL)
    new_ckv_p = cat("ockv_p")[None]
    new_kr_p = cat("okr_p")[None]
    new_conv_p = cat("oconv_p")[None]
    new_ckv_s = cat("ockv_s").reshape(32, DEC_SEQ, 128)[None]
    new_kr_s = cat("okr_s").reshape(32, DEC_SEQ, 32)[None]
    new_conv_s = cat("oconv_s")[None]
    return (y_prompt, y_sample, new_ckv_p, new_kr_p, new_conv_p, new_ckv_s, new_kr_s, new_conv_s)
```

```python
import numpy as np
import sys as _sys
from contextlib import ExitStack
import concourse.bass as bass
import concourse.mybir as mybir
from concourse.bass_utils import run_bass_kernel_spmd

F32 = mybir.dt.float32
BF16 = mybir.dt.bfloat16
ALU = mybir.AluOpType
AF = mybir.ActivationFunctionType

NCORES = 8
D_MODEL = 1024
SEQ = 2048
PAST = 1024
DEC_SEQ = 32
NH = 8
EPS = 1e-6
ATT_SCALE = 96.0 ** -0.5
TILE = 512
WINDOW = 400
XLAT = 220.0
USE_BLEVEL = True
SAMPLE_POS = 'end'
KV_PSB = False
PRIO_NOISE = 0.0
PRIO_SEED = 0
DMA_ISSUE_NS = 500.0
ENG_WINDOW = {}
import os as _os0
STRICT = True
PE_RELAX = True
NSEQ_P = 2
NSEQ_S = 4


class Op:
    __slots__ = ("eng", "fn", "idx", "deps", "dma", "semkey", "signal", "val", "waits", "gidx", "cost", "fin", "done",
                 "succ", "npend", "ready", "tag", "blevel")

    def __init__(self, eng, fn, idx, dma, semkey):
        self.eng, self.fn, self.idx, self.dma, self.semkey = eng, fn, idx, dma, semkey
        self.cost = 500.0
        self.fin = 0.0
        self.done = False
        self.deps = {}
        self.signal = False
        self.val = 0
        self.waits = []


class Sched:
    ENGS = ("pe", "act", "dve", "pool", "sp")

    def __init__(self):
        self.ops = {e: [] for e in self.ENGS}
        self.lastw = {}
        self.readers = {}
        self.dma_count = {}
        self.gcount = 0

    def add(self, eng, fn, reads=(), writes=(), dma=False, semkey=None, cost=None, n=512):
        op = Op(eng, fn, len(self.ops[eng]), dma, semkey)
        op.gidx = self.gcount
        self.gcount += 1
        op.tag = _sys._getframe(1).f_lineno if not dma else _sys._getframe(2).f_lineno
        if cost is None:
            if dma:
                cost = 3500.0
            elif eng == "act":
                cost = 230.0 + 0.85 * n
            elif eng == "dve":
                cost = 120.0 + 1.05 * n
            elif eng == "pool":
                cost = 350.0 + 1.7 * n
            else:
                cost = 60.0 + 0.45 * n
        op.cost = float(cost)
        for k in reads:
            w = self.lastw.get(k)
            if w is not None:
                op.deps[w] = "raw"
            if k.startswith("ps"):
                for r in self.readers.get(k, ()):
                    if r.eng != eng and r not in op.deps:
                        op.deps[r] = "rr"
        for k in writes:
            w = self.lastw.get(k)
            if w is not None and w not in op.deps:
                op.deps[w] = "waw"
            for r in self.readers.get(k, ()):
                if r is not op and r not in op.deps:
                    op.deps[r] = "war"
        for k in reads:
            self.readers.setdefault(k, []).append(op)
        for k in writes:
            self.lastw[k] = op
            self.readers[k] = []
        if dma:
            op.signal = True
        self.ops[eng].append(op)
        return op

    def reorder(self, window=WINDOW, xlat=XLAT):
        allops = [op for e in self.ENGS for op in self.ops[e]]
        for op in allops:
            op.succ = []
            op.done = False
        for op in allops:
            op.npend = len(op.deps)
            for d in op.deps:
                d.succ.append(op)
            op.ready = 0.0
        order = sorted(allops, key=lambda o: o.gidx)
        for op in reversed(order):
            bl = 0.0
            for sc in op.succ:
                if sc.blevel > bl:
                    bl = sc.blevel
            op.blevel = bl + (op.cost if not op.dma else op.cost)
        if PRIO_NOISE > 0:
            import random as _rnd
            rng = _rnd.Random(PRIO_SEED)
            for op in order:
                op.blevel *= (1.0 + PRIO_NOISE * (rng.random() - 0.5))
        pending = {e: list(self.ops[e]) for e in self.ENGS}
        free = {e: 0.0 for e in self.ENGS}
        new = {e: [] for e in self.ENGS}
        remaining = len(allops)
        while remaining:
            best = None
            for e in self.ENGS:
                lst = pending[e]
                fe = free[e]
                cand = None
                for op in lst[:(ENG_WINDOW.get(e, window))]:
                    if op.npend:
                        continue
                    st_ = op.ready if op.ready > fe else fe
                    if cand is None:
                        cand = (st_, op)
                    else:
                        cs, cop = cand
                        if st_ <= fe and cs <= fe:
                            if USE_BLEVEL and op.blevel > cop.blevel:
                                cand = (st_, op)
                        elif st_ < cs:
                            cand = (st_, op)
                if cand is None:
                    continue
                st_, op = cand
                if best is None or st_ < best[0] - 1e-9 or (abs(st_ - best[0]) <= 1e-9 and op.gidx < best[1].gidx):
                    best = (st_, op)
            st_, op = best
            e = op.eng
            pending[e].remove(op)
            new[e].append(op)
            if op.dma:
                free[e] = st_ + DMA_ISSUE_NS
                op.fin = st_ + op.cost
            else:
                free[e] = st_ + op.cost
                op.fin = free[e]
            op.done = True
            for sc in op.succ:
                sc.npend -= 1
                r = op.fin + (xlat if (sc.eng != e or op.dma) else 0.0)
                if r > sc.ready:
                    sc.ready = r
            remaining -= 1
        self.ops = new
        for e in self.ENGS:
            for i, op in enumerate(new[e]):
                op.idx = i
        self.makespan = max(op.fin for op in allops)

    def resolve(self):
        self.reorder()
        for op in self.ops["sp"]:
            n = self.dma_count.get(op.semkey, 0) + 1
            self.dma_count[op.semkey] = n
            op.val = 16 * n
        for eng in self.ENGS:
            for op in self.ops[eng]:
                for d, kind in op.deps.items():
                    if d.dma:
                        op.waits.append(d)
                        continue
                    if d.eng == op.eng:
                        if op.dma:
                            continue
                        if STRICT and not (eng == "pe" and PE_RELAX):
                            d.signal = True
                            op.waits.append(d)
                            continue
                        if eng == "pe":
                            continue
                        if kind == "raw" and (op.idx - d.idx) <= 2:
                            d.signal = True
                            op.waits.append(d)
                        continue
                    d.signal = True
                    op.waits.append(d)
        for eng in self.ENGS:
            c = 0
            for op in self.ops[eng]:
                if op.dma:
                    continue
                if op.signal:
                    c += 1
                    op.val = c

    def emit(self, nc, engines, sems, dma_sems):
        for eng in self.ENGS:
            pass

    def run_engine(self, eng, e, sems, dma_sems, final_wait=False):
        waited = {}
        for op in self.ops[eng]:
            need = {}
            for d in op.waits:
                if d.dma:
                    key = ("dma", d.semkey)
                    sem = dma_sems[d.semkey]
                else:
                    key = ("eng", d.eng)
                    sem = sems[d.eng]
                if key not in need or need[key][1] < d.val:
                    need[key] = (sem, d.val)
            for key, (sem, val) in need.items():
                if waited.get(key, 0) >= val:
                    continue
                e.wait_ge(sem, val)
                waited[key] = val
            ins = op.fn(e)
            if op.dma:
                ins.then_inc(dma_sems[op.semkey], 16)
            elif op.signal:
                ins.then_inc(sems[eng], 1)
        if final_wait:
            for k, n in self.dma_count.items():
                e.wait_ge(dma_sems[k], 16 * n)


def build_program(do_prompt=True, debug=False):
    nc = bass.Bass("TRN2", target_bir_lowering=False, dynamic_dma_scratch_size=256)
    S = Sched()

    def din(name, shape, dt=F32):
        return nc.dram_tensor(name, list(shape), dt, kind="ExternalInput").ap()

    def dout(name, shape, dt=F32):
        return nc.dram_tensor(name, list(shape), dt, kind="ExternalOutput").ap()

    xp = din("xp", [NSEQ_P, SEQ, D_MODEL])
    xs = din("xs", [128, D_MODEL])
    cckv = din("cckv", [NSEQ_S, PAST, 128])
    ckr = din("ckr", [NSEQ_S, PAST, 32])
    sconv = din("sconv", [NSEQ_S, 30, 512])
    wbig = din("wbig", [128, 8, 2048])
    wsmall = din("wsmall", [128, 8, 448])
    wqb = din("wqb", [128, 2, 1536])
    wkv = din("wkv", [128, 1024])
    wout = din("wout", [128, 8, 1024])
    convwp = din("convwp", [128, 4, 4, 8])
    estack = din("estack", [128, 32])
    vecs = din("vecs", [128, 32])
    gpost_b = din("gpost_b", [128, 1024])
    gkva_b = din("gkva_b", [128, 128])
    ident = din("ident", [128, 128])
    costm_d = din("costm", [128, 17, 32])
    sintm_d = din("sintm", [128, 17, 32])
    cosT_d = din("cosT", [32, SEQ + 128])
    sinT_d = din("sinT", [32, SEQ + 128])

    yp = dout("yp", [NSEQ_P, SEQ, D_MODEL])
    ys = dout("ys", [128, D_MODEL])
    ockv_p = dout("ockv_p", [NSEQ_P, SEQ, 128])
    okr_p = dout("okr_p", [NSEQ_P, SEQ, 32])
    oconv_p = dout("oconv_p", [NSEQ_P, 30, 512])
    ockv_s = dout("ockv_s", [128, 128])
    okr_s = dout("okr_s", [128, 32])
    oconv_s = dout("oconv_s", [NSEQ_S, 30, 512])

    wbigD = nc.dram_tensor("wbigD", [16, 128, 1024], BF16, kind="Internal").ap()
    gluD = nc.dram_tensor("gluD", [4, 128, 544], BF16, kind="Internal").ap()

    es = ExitStack()

    def sb(name, shape, dt):
        return es.enter_context(nc.sbuf_tensor("s_" + name, list(shape), dt))

    def ps(name, shape, dt):
        return es.enter_context(nc.psum_tensor("p_" + name, list(shape), dt))

    Wsmall = sb("Wsmall", [128, 8, 448], BF16)
    Wqb = sb("Wqb", [128, 2, 1536], BF16)
    Wkv = sb("Wkv", [128, 1024], BF16)
    Wout = sb("Wout", [128, 8, 1024], BF16)
    KT = sb("KT", [128, 8, 2112], BF16)
    V = sb("V", [128, 18, 512], BF16)
    vones = sb("vones", [128, 64], BF16)
    gpostb = sb("gpostb", [128, 1024], F32)
    gkvab = sb("gkvab", [128, 128], F32)
    identf = sb("identf", [128, 128], F32)
    identb = sb("identb", [128, 128], BF16)
    onesS = sb("onesS", [128, 128], BF16)
    neghalf = sb("neghalf", [128, 512], F32)
    costm = sb("costm", [128, 17, 32], F32)
    sintm = sb("sintm", [128, 17, 32], F32)
    vec = sb("vec", [128, 32], F32)
    vech = sb("vech", [128, 32], F32)
    convwt = sb("convwt", [128, 4, 4, 8], F32)
    estk = sb("estk", [128, 32], F32)
    Wp = sb("Wp", [128, 4, 4, 8, 32], BF16)
    NWB = 3
    Wb = [sb(f"Wb{i}", [128, 8, 128], BF16) for i in range(NWB)]
    NGT = 2
    Gt = [sb(f"Gt{i}", [128, 4, 520], BF16) for i in range(NGT)]
    NXS = 2
    xsb = [sb(f"xsb{i}", [128, 1024], F32) for i in range(NXS)]
    hb = [sb(f"hb{i}", [128, 1024], BF16) for i in range(2)]
    hT = sb("hT", [128, 8, 512], BF16)
    st = sb("st", [128, 16], F32)
    qcn = [sb(f"qcn{i}", [128, 256], BF16) for i in range(2)]
    ckvsb = [sb(f"ckvsb{i}", [128, 128], F32) for i in range(2)]
    krt = [sb(f"krt{i}", [128, 96], F32) for i in range(2)]
    rt = [sb(f"rt{i}", [128, 32], F32) for i in range(2)]
    qcT = sb("qcT", [128, 2, 512], BF16)
    ckvT = sb("ckvT", [128, 512], BF16)
    krT = sb("krT", [128, 512], BF16)
    glu = sb("glu", [128, 4, 544], BF16)
    glu32 = sb("glu32", [128, 4, 128], F32)
    sg = sb("sg", [128, 4, 512], BF16)
    sga = sb("sga", [128, 4, 512], BF16)
    xres = [sb(f"xres{i}", [128, 1024], F32) for i in range(2)]
    junk = sb("junk", [128, 1024], BF16)
    rlb = sb("rlb", [128, 512], F32)
    o1b = sb("o1b", [128, 512], F32)
    ybf = sb("ybf", [128, 4, 512], BF16)
    ysq = [sb(f"ysq{i}", [128, 512], BF16) for i in range(2)]
    rstdln = sb("rstdln", [128, 512], F32)
    NPOOL = 5
    fpool = [sb(f"fp{i}", [128, 512], F32) for i in range(NPOOL)]
    mixedT = sb("mixedT", [128, 8, 512], BF16)
    QT = sb("QT", [128, 8, 512], BF16)
    cosT = sb("cosT", [128, 512], F32)
    sinT = sb("sinT", [128, 512], F32)
    NPT = 3
    PT = [sb(f"PT{i}", [128, 512], BF16) for i in range(NPT)]
    tmpo = sb("tmpo", [128, 1024], F32)
    gluS = sb("gluS", [128, 4, NSEQ_S, 64], BF16)
    osb = sb("osb", [128, 512], F32)
    psT = ps("psT", [128, 1024], BF16)
    psS = ps("psS", [128, 512], F32)
    psB = [ps(f"psB{i}", [128, 512], F32) for i in range(2)]
    psA = [ps(f"psA{i}", [128, 512], F32) for i in range(2)]
    psO = ps("psO", [128, 1024], F32)

    cnt = {"B": 0, "A": 0, "fp": 0, "xs": 0, "h": 0, "PT": 0, "Wb": 0, "Dc": 0, "ysq": 0, "tm": 0, "xr": 0, "G": 0}

    def nxt(kind, n):
        i = cnt[kind] % n
        cnt[kind] += 1
        return i

    def nB():
        i = nxt("B", 2)
        return psB[i], f"psB{i}"

    psA3 = [psA[0], psA[1], psS]

    def nA():
        i = nxt("A", 3)
        return psA3[i], f"psA{i}"

    def nfp():
        i = nxt("fp", NPOOL)
        return fpool[i], f"fp{i}"

    A = S.add

    def dma(fn, reads, writes, semkey, cost=None):
        return S.add("sp", fn, reads, writes, dma=True, semkey=semkey, cost=cost)

    dma(lambda e: e.dma_start(out=vec[:], in_=vecs[:, :]), [], ["vec"], "c_vec")
    dma(lambda e: e.dma_start(out=identf[:], in_=ident[:, :]), [], ["identf"], "c_id")
    dma(lambda e: e.dma_start(out=convwt[:], in_=convwp[:, :, :, :]), [], ["convwt"], "c_cw")
    dma(lambda e: e.dma_start(out=estk[:], in_=estack[:, :]), [], ["estk"], "c_es")
    GLUK = [f"glu{c_}" for c_ in range(4)]
    A("pool", lambda e: e.memset(glu[:, :, 542:544], 0.0), [], GLUK)
    A("pool", lambda e: e.memset(gluS[:, :, :, 62:64], 0.0), [], GLUK)
    dma(lambda e: e.dma_start(out=gpostb[:], in_=gpost_b[:, :]), [], ["gpostb"], "c_gp")
    dma(lambda e: e.dma_start(out=gkvab[:], in_=gkva_b[:, :]), [], ["gkvab"], "c_gk")
    dma(lambda e: e.dma_start(out=costm[:], in_=costm_d[:, :, :]), [], ["costm"], "c_ct")
    dma(lambda e: e.dma_start(out=sintm[:], in_=sintm_d[:, :, :]), [], ["sintm"], "c_st")

    A("dve", lambda e: e.tensor_scalar(out=vech[:], in0=vec[:], scalar1=0.5, scalar2=None, op0=ALU.mult),
      ["vec"], ["vech"])
    A("dve", lambda e: e.tensor_copy(out=identb[:], in_=identf[:]), ["identf"], ["identb"])
    A("pool", lambda e: e.memset(onesS[:], 1.0 / 512.0), [], ["onesS"])
    A("pool", lambda e: e.memset(neghalf[:], -0.5), [], ["neghalf"])
    A("pool", lambda e: e.memset(vones[:], 1.0), [], ["vones"])
    A("pool", lambda e: e.memset(krt[0][:], 0.0), [], ["krt0"])
    A("pool", lambda e: e.memset(krt[1][:], 0.0), [], ["krt1"])

    cast_engs = ["dve", "act"]
    ccnt = [0]

    def cast_scaled(out_ap, in_ap, scal_ap, reads, writes, n=1024):
        eng = cast_engs[ccnt[0] % 2]
        ccnt[0] += 1
        if eng == "act":
            if scal_ap is None:
                A("act", lambda e: e.activation(out=out_ap, in_=in_ap, func=AF.Copy), reads, writes, n=n)
            else:
                A("act", lambda e: e.activation(out=out_ap, in_=in_ap, func=AF.Copy, scale=scal_ap), reads, writes, n=n)
        else:
            if scal_ap is None:
                A(eng, lambda e: e.tensor_copy(out=out_ap, in_=in_ap), reads, writes, cost=2100.0 * n / 1024)
            else:
                A(eng, lambda e: e.tensor_scalar(out=out_ap, in0=in_ap, scalar1=scal_ap, scalar2=None,
                                                 op0=ALU.mult), reads, writes, cost=2100.0 * n / 1024)

    stg = [(xres[0], ["xres0"]), (xres[1], ["xres1"]), (tmpo, ["tmpo0", "tmpo1"])]
    stgb = [(mixedT[:, 2 * i:2 * i + 2, :].rearrange("p a b -> p (a b)"), [f"mixedT{2 * i}", f"mixedT{2 * i + 1}"]) for i in range(4)]
    scnt = {"f": 0, "b": 0}

    def nstg():
        i = scnt["f"] % 3
        scnt["f"] += 1
        return stg[i][0], stg[i][1], i

    def nstgb():
        i = scnt["b"] % 4
        scnt["b"] += 1
        return stgb[i][0], stgb[i][1], i

    for piece in range(4):
        sf, kf, si = nstg()
        dma(lambda e, sf=sf, piece=piece: e.dma_start(out=sf[:, 0:896].rearrange("p (k n) -> p k n", k=2),
                                                        in_=wsmall[:, 2 * piece:2 * piece + 2, :]),
            [], kf, f"stg{si}")
        for j in range(2):
            kc = 2 * piece + j
            cast_scaled(Wsmall[:, kc, :], sf[:, j * 448:(j + 1) * 448], vec[:, kc:kc + 1],
                        kf + ["vec"], [f"Wsmall{kc}"], n=448)
    WSMALL_KEYS = [f"Wsmall{kc}" for kc in range(8)]
    WQB_KEYS = [f"Wqb{a_}{b_}" for a_ in range(2) for b_ in range(2)]
    WOUT_KEYS = [f"Wout{kc}" for kc in range(8)]

    def setup_rest():
        for half in range(2):
            for kc in range(8):
                sf, kf, si = nstg()
                sbf, kbf, bi = nstgb()
                dma(lambda e, sf=sf, kc=kc, half=half: e.dma_start(out=sf[:, :], in_=wbig[:, kc, half * 1024:(half + 1) * 1024]),
                    [], kf, f"stg{si}")
                cast_scaled(sbf, sf[:, :], vech[:, kc:kc + 1], kf + ["vech"], kbf)
                dma(lambda e, sbf=sbf, kc=kc, half=half: e.dma_start(
                    out=wbigD[half * 8:(half + 1) * 8, :, kc * 128:(kc + 1) * 128].rearrange("c p n -> p c n"),
                    in_=sbf.rearrange("p (c n) -> p c n", c=8)),
                    kbf, [f"wbigD{half}_{kc}"], f"stgbst{bi}")
        for c in range(4):
            for g in range(4):
                for r in range(8):
                    A("dve", lambda e, c=c, g=g, r=r: e.tensor_scalar(out=Wp[:, c, g, r, :], in0=estk[:], scalar1=convwt[:, c, g, r:r + 1],
                                                                      scalar2=None, op0=ALU.mult),
                      ["estk", "convwt"], [f"Wp{c}"], n=32)
        for kc2 in range(2):
            for half in range(2):
                sf, kf, si = nstg()
                dma(lambda e, sf=sf, kc2=kc2, half=half: e.dma_start(out=sf[:, 0:768], in_=wqb[:, kc2, half * 768:(half + 1) * 768]),
                    [], kf, f"stg{si}")
                cast_scaled(Wqb[:, kc2, half * 768:(half + 1) * 768], sf[:, 0:768], vec[:, 8 + kc2:9 + kc2],
                            kf + ["vec"], [f"Wqb{kc2}{half}"], n=768)
        sf, kf, si = nstg()
        dma(lambda e, sf=sf: e.dma_start(out=sf[:, :], in_=wkv[:, :]), [], kf, f"stg{si}")
        cast_scaled(Wkv[:], sf[:, :], None, kf, ["Wkv"])
        for kc in range(8):
            sf, kf, si = nstg()
            dma(lambda e, sf=sf, kc=kc: e.dma_start(out=sf[:, :], in_=wout[:, kc, :]), [], kf, f"stg{si}")
            cast_scaled(Wout[:, kc, :], sf[:, :], None, kf, [f"Wout{kc}"])

    def rms_rstd(src_ap, src_keys, width, col, junk_ap, junk_key):
        sc = float(width) ** -0.5
        A("act", lambda e: e.activation(out=junk_ap, in_=src_ap, func=AF.Square, scale=sc,
                                        accum_out=st[:, col:col + 1]),
          src_keys, [junk_key, f"st{col}"], n=width)
        A("pool", lambda e: e.tensor_scalar(out=st[:, col:col + 1], in0=st[:, col:col + 1], scalar1=EPS,
                                            scalar2=None, op0=ALU.add),
          [f"st{col}"], [f"st{col}"], cost=300)
        A("pool", lambda e: e.tensor_tensor(out=st[:, col:col + 1], in0=st[:, col:col + 1],
                                            in1=neghalf[:, 0:1], op=ALU.pow),
          [f"st{col}", "neghalf"], [f"st{col}"], cost=550)

    def stage_AB(x_src, nsub, tm_idx, ckv_dst, kr_dst):
        for sub in range(nsub):
            ts = slice(sub * 128, (sub + 1) * 128)
            xi = nxt("xs", NXS)
            hi = nxt("h", 2)
            dma(lambda e, xi=xi, sub=sub: e.dma_start(out=xsb[xi][:], in_=x_src(sub)), [], [f"xsb{xi}"], f"xsb{xi}")
            rms_rstd(xsb[xi][:], [f"xsb{xi}"], 1024, 0, hb[hi][:], f"hb{hi}")
            A("dve", lambda e, xi=xi, hi=hi: e.tensor_scalar(out=hb[hi][:], in0=xsb[xi][:], scalar1=st[:, 0:1],
                                                            scalar2=None, op0=ALU.mult),
              [f"xsb{xi}", "st0"], [f"hb{hi}"], n=1024)

            def tr(e, hi=hi):
                ins = None
                for k in range(8):
                    ins = e.transpose(out=psT[:, k * 128:(k + 1) * 128], in_=hb[hi][:, k * 128:(k + 1) * 128],
                                      identity=identb[:])
                return ins
            A("pe", tr, [f"hb{hi}", "identb"], ["psT"], cost=700)
            A("dve", lambda e, ts=ts: e.tensor_copy(out=hT[:, :, ts], in_=psT[:].rearrange("p (k n) -> p k n", k=8)),
              ["psT"], ["hT"], n=1024)

            pSm, kSm = nB()

            def smallp(e, ts=ts, pSm=pSm):
                ins = None
                for k in range(8):
                    ins = e.matmul(pSm[:, 0:448], lhsT=hT[:, k, ts], rhs=Wsmall[:, k, :], start=(k == 0), stop=(k == 7))
                return ins
            A("pe", smallp, ["hT"] + WSMALL_KEYS, [kSm], cost=1800)
            qi = sub % 2
            rms_rstd(pSm[:, 0:256], [kSm], 256, 1, hb[hi][:, 0:256], f"hb{hi}")
            A("dve", lambda e, qi=qi, pSm=pSm: e.tensor_scalar(out=qcn[qi][:], in0=pSm[:, 0:256], scalar1=st[:, 1:2],
                                                      scalar2=None, op0=ALU.mult),
              [kSm, "st1"], [f"qcn{qi}"], n=256)
            rms_rstd(pSm[:, 256:384], [kSm], 128, 2, hb[hi][:, 256:384], f"hb{hi}")
            A("dve", lambda e, qi=qi, pSm=pSm: e.scalar_tensor_tensor(out=ckvsb[qi][:], in0=pSm[:, 256:384], scalar=st[:, 2:3],
                                                             in1=gkvab[:], op0=ALU.mult, op1=ALU.mult),
              [kSm, "st2", "gkvab"], [f"ckvsb{qi}"], n=128)
            dma(lambda e, qi=qi, sub=sub: e.dma_start(out=ckv_dst(sub), in_=ckvsb[qi][:]), [f"ckvsb{qi}"], [], f"ckvst{qi}")
            ti = tm_idx(sub)
            A("dve", lambda e, qi=qi, ti=ti, pSm=pSm: e.tensor_tensor(out=rt[0][:], in0=pSm[:, 384:416], in1=costm[:, ti, :], op=ALU.mult),
              [kSm, "costm"], ["rt0"], n=32)
            A("dve", lambda e, qi=qi, ti=ti, pSm=pSm: e.tensor_tensor(out=rt[1][:], in0=pSm[:, 416:448], in1=sintm[:, ti, :], op=ALU.mult),
              [kSm, "sintm"], ["rt1"], n=32)
            A("pool", lambda e, qi=qi, pSm=pSm: e.tensor_tensor(out=krt[qi][:, 64:96], in0=rt[0][:], in1=rt[1][:], op=ALU.add),
              ["rt0", "rt1"], [f"krt{qi}"], n=32)
            dma(lambda e, qi=qi, sub=sub: e.dma_start(out=kr_dst(sub), in_=krt[qi][:, 64:96]), [f"krt{qi}"], [], f"krst{qi}")

            def trq(e, qi=qi):
                e.transpose(out=psT[:, 0:128], in_=qcn[qi][:, 0:128], identity=identb[:])
                return e.transpose(out=psT[:, 128:256], in_=qcn[qi][:, 128:256], identity=identb[:])
            A("pe", trq, [f"qcn{qi}", "identb"], ["psT"], cost=200)
            A("dve", lambda e, ts=ts: e.tensor_copy(out=qcT[:, :, ts], in_=psT[:, 0:256].rearrange("p (k n) -> p k n", k=2)),
              ["psT"], ["qcT"], n=256)
            pB, kB = nB()

            def trk(e, qi=qi, pB=pB):
                e.transpose(out=pB[:, 0:128], in_=ckvsb[qi][:], identity=identf[:])
                return e.transpose(out=pB[0:96, 128:256], in_=krt[qi][:], identity=identf[:])
            A("pe", trk, [f"ckvsb{qi}", f"krt{qi}", "identf"], [kB], cost=300)
            A("dve", lambda e, ts=ts, pB=pB: e.tensor_copy(out=ckvT[:, ts], in_=pB[:, 0:128]), [kB], ["ckvT"], n=128)
            A("dve", lambda e, ts=ts, pB=pB: e.tensor_copy(out=krT[64:96, ts], in_=pB[64:96, 128:256]), [kB], ["krT"], n=128)

    def big_chunk(cc, ntok):
        wi = nxt("Wb", NWB)
        dma(lambda e, wi=wi, cc=cc: e.dma_start(out=Wb[wi][:].rearrange("p k n -> p (k n)"), in_=wbigD[cc, :, :]),
            [f"wbigD{cc // 8}_{k_}" for k_ in range(8)], [f"Wb{wi}"], f"Wb{wi}")
        pB, kB = nB()

        def mm(e, wi=wi, pB=pB):
            ins = None
            for k in range(8):
                ins = e.matmul(pB[:, 0:ntok], lhsT=Wb[wi][:, k, :], rhs=hT[:, k, 0:ntok], start=(k == 0), stop=(k == 7))
            return ins
        A("pe", mm, [f"Wb{wi}", "hT"], [kB], cost=8 * (ntok * 0.45 + 20))
        return pB, kB

    def stage_C1(ntok, glu_dst, glu32_dst):
        for c in range(4):
            pBb, kBb = big_chunk(4 + c, ntok)
            th, kth = nfp()
            A("act", lambda e, pBb=pBb, th=th: e.activation(out=th[:, 0:ntok], in_=pBb[:, 0:ntok], func=AF.Tanh),
              [kBb], [kth], n=ntok)
            pBa, kBa = big_chunk(c, ntok)
            for (dst, dkeys) in glu_dst(c):
                A("dve", lambda e, pBa=pBa, th=th, dst=dst: e.scalar_tensor_tensor(
                    out=dst[1], in0=th[:, dst[0]], scalar=1.0, in1=pBa[:, dst[0]], op0=ALU.add, op1=ALU.mult),
                  [kBa, kth], dkeys, n=ntok // len(glu_dst(c)))
            if glu32_dst is not None:
                d32 = glu32_dst(c)
                A("dve", lambda e, pBa=pBa, th=th, d32=d32: e.scalar_tensor_tensor(
                    out=d32[1], in0=th[:, d32[0]], scalar=1.0, in1=pBa[:, d32[0]], op0=ALU.add, op1=ALU.mult),
                  [kBa, kth], ["glu32"], n=128)
        gates(8, ntok)

    def gates(cc0, ntok, dst=None, dkey="sg"):
        dst = sg if dst is None else dst
        for c in range(4):
            pBg, kBg = big_chunk(cc0 + c, ntok)
            th, kth = nfp()
            A("act", lambda e, pBg=pBg, th=th: e.activation(out=th[:, 0:ntok], in_=pBg[:, 0:ntok], func=AF.Tanh),
              [kBg], [kth], n=ntok)
            A("dve", lambda e, pBg=pBg, th=th, c=c: e.scalar_tensor_tensor(
                out=dst[:, c, 0:ntok], in0=th[:, 0:ntok], scalar=1.0, in1=pBg[:, 0:ntok], op0=ALU.add, op1=ALU.mult),
              [kBg, kth], [dkey], n=ntok)

    def stage_D(ntok, conv_mm, g_dst, g_src, conv_cost, g_store, gw, nparts):
        pM, kM = nA()
        pQ, kQ = psB[0], "psB0"
        for c in range(4):
            gi = nxt("Dc", NGT)
            dma(lambda e, c=c: e.dma_start(out=gluD[c, :, 0:gw], in_=g_store(c)), [f"glu{c}"], [f"gluD{c}"], f"gluDst{c}", cost=3000)
            for sh in range(4):
                for part in range(nparts):
                    dma(lambda e, gi=gi, c=c, sh=sh, part=part: e.dma_start(
                        out=g_dst(Gt[gi], sh, part), in_=g_src(gluD[c, :, 0:gw].rearrange("(g q) t -> q g t", g=4), sh, part)),
                        [f"gluD{c}"], [f"Gt{gi}_{sh}_{part}"], f"Gt{gi}_{sh}_{part}", cost=3500)
            pB, kB = psB[1], "psB1"
            A("pe", lambda e, gi=gi, c=c, pB=pB: conv_mm(e, Gt[gi], c, pB),
              [f"Gt{gi}_{s_}_{p_}" for s_ in range(4) for p_ in range(nparts)] + [f"Wp{c}"], [kB], cost=conv_cost)
            yi = nxt("ysq", 2)
            A("act", lambda e, pB=pB, c=c: e.activation(out=ybf[:, c, 0:ntok], in_=pB[:, 0:ntok], func=AF.Identity,
                                                         bias=vec[:, 10 + c:11 + c]),
              [kB, "vec"], [f"ybf{c}"], n=ntok)
            A("act", lambda e, pB=pB, c=c, yi=yi: e.activation(out=ysq[yi][:, 0:ntok], in_=pB[:, 0:ntok], func=AF.Square,
                                                                bias=vec[:, 10 + c:11 + c]),
              [kB, "vec"], [f"ysq{yi}"], n=ntok)
            A("pe", lambda e, c=c: e.matmul(pM[:, 0:ntok], lhsT=onesS[:], rhs=ybf[:, c, 0:ntok], start=(c == 0), stop=(c == 3)),
              [f"ybf{c}", "onesS"], [kM], cost=ntok * 0.45 + 20)
            A("pe", lambda e, c=c, yi=yi: e.matmul(pQ[:, 0:ntok], lhsT=onesS[:], rhs=ysq[yi][:, 0:ntok], start=(c == 0), stop=(c == 3)),
              [f"ysq{yi}", "onesS"], [kQ], cost=ntok * 0.45 + 20)
        m2, km2 = nfp()
        A("act", lambda e: e.activation(out=m2[:, 0:ntok], in_=pM[:, 0:ntok], func=AF.Square), [kM], [km2], n=ntok)
        A("dve", lambda e: e.scalar_tensor_tensor(out=rstdln[:, 0:ntok], in0=pQ[:, 0:ntok], scalar=EPS, in1=m2[:, 0:ntok],
                                                  op0=ALU.add, op1=ALU.subtract),
          [kQ, km2], ["rstdln"], n=ntok)
        A("act", lambda e: e.activation(out=rstdln[:, 0:ntok], in_=rstdln[:, 0:ntok], func=AF.Ln), ["rstdln"], ["rstdln"], cost=1600 + ntok)
        A("act", lambda e: e.activation(out=rstdln[:, 0:ntok], in_=rstdln[:, 0:ntok], func=AF.Exp, scale=-0.5), ["rstdln"], ["rstdln"], n=ntok)
        for c in range(4):
            t1, k1 = nfp()
            A("dve", lambda e, c=c, t1=t1: e.tensor_tensor(out=t1[:, 0:ntok], in0=ybf[:, c, 0:ntok], in1=pM[:, 0:ntok], op=ALU.subtract),
              [f"ybf{c}", kM], [k1], n=ntok)
            A("pool", lambda e, t1=t1: e.tensor_tensor(out=t1[:, 0:ntok], in0=t1[:, 0:ntok], in1=rstdln[:, 0:ntok], op=ALU.mult),
              [k1, "rstdln"], [k1], n=ntok)
            th, kth = nfp()
            A("act", lambda e, c=c, t1=t1, th=th: e.activation(out=th[:, 0:ntok], in_=t1[:, 0:ntok], func=AF.Tanh,
                                                                scale=vech[:, 14 + c:15 + c], bias=vech[:, 18 + c:19 + c]),
              [k1, "vech"], [kth], cost=1600 + ntok)
            zz, kzz = nfp()
            A("dve", lambda e, c=c, t1=t1, zz=zz: e.tensor_scalar(out=zz[:, 0:ntok], in0=t1[:, 0:ntok], scalar1=vech[:, 14 + c:15 + c],
                                                                  scalar2=vech[:, 18 + c:19 + c], op0=ALU.mult, op1=ALU.add),
              [k1, "vech"], [kzz], n=ntok)
            A("dve", lambda e, th=th, zz=zz: e.scalar_tensor_tensor(out=zz[:, 0:ntok], in0=th[:, 0:ntok], scalar=1.0, in1=zz[:, 0:ntok],
                                                                    op0=ALU.add, op1=ALU.mult),
              [kth, kzz], [kzz], n=ntok)
            A("pool", lambda e, c=c, zz=zz: e.tensor_tensor(out=mixedT[:, c, 0:ntok], in0=zz[:, 0:ntok], in1=sg[:, c, 0:ntok], op=ALU.mult),
              [kzz, "sg"], [f"mixedT{c}"], n=ntok)

    def stage_E_q(ntok, rope_col0):
        dma(lambda e: e.dma_start(out=cosT[64:96, 0:ntok], in_=cosT_d[:, rope_col0:rope_col0 + ntok]), [], ["cosT"], "cosT")
        dma(lambda e: e.dma_start(out=sinT[64:96, 0:ntok], in_=sinT_d[:, rope_col0:rope_col0 + ntok]), [], ["sinT"], "sinT")
        for h in range(NH):
            pa, ka = nB()
            pb, kb = nB()

            def mmq(e, h=h, pa=pa, pb=pb):
                ins = None
                for v, pp in ((0, pa), (1, pb)):
                    for k2 in range(2):
                        off = (h * 2 + v) * 96
                        ins = e.matmul(pp[0:96, 0:ntok], lhsT=Wqb[:, k2, off:off + 96], rhs=qcT[:, k2, 0:ntok],
                                       start=(k2 == 0), stop=(k2 == 1))
                return ins
            A("pe", mmq, WQB_KEYS + ["qcT"], [ka, kb], cost=4 * (ntok * 0.45 + 20))
            A("act", lambda e, h=h, pa=pa: e.activation(out=QT[0:64, h, 0:ntok], in_=pa[0:64, 0:ntok], func=AF.Copy),
              [ka], [f"QT{h}n"], n=ntok)
            r1, k1 = nfp()
            r2, k2_ = nfp()
            A("dve", lambda e, pa=pa, r1=r1: e.tensor_tensor(out=r1[64:96, 0:ntok], in0=pa[64:96, 0:ntok], in1=cosT[64:96, 0:ntok], op=ALU.mult),
              [ka, "cosT"], [k1], n=ntok)
            A("dve", lambda e, pb=pb, r2=r2: e.tensor_tensor(out=r2[64:96, 0:ntok], in0=pb[64:96, 0:ntok], in1=sinT[64:96, 0:ntok], op=ALU.mult),
              [kb, "sinT"], [k2_], n=ntok)
            A("pool", lambda e, h=h, r1=r1, r2=r2: e.tensor_tensor(out=QT[64:96, h, 0:ntok], in0=r1[64:96, 0:ntok], in1=r2[64:96, 0:ntok], op=ALU.add),
              [k1, k2_], [f"QT{h}r"], n=ntok)

    def ktk(h, c0, c1):
        ks = []
        if c0 < 1056:
            ks.append(f"KT{h}a")
        if c1 > 1056:
            ks.append(f"KT{h}b")
        return ks

    def kv_project(ntok, key0, src_T, src_key, kr_src, kr_key, vb0=None, bcast=False, bsem="KTrb"):
        vb0 = key0 // 128 if vb0 is None else vb0
        for hp in range(4):
            pa, ka = (nB() if KV_PSB else nA())
            A("pe", lambda e, hp=hp, pa=pa: e.matmul(pa[:, 0:ntok], lhsT=Wkv[:, hp * 128:(hp + 1) * 128], rhs=src_T[:, 0:ntok],
                                                     start=True, stop=True),
              ["Wkv", src_key], [ka], cost=ntok * 0.45 + 20)
            A("dve", lambda e, hp=hp, pa=pa: e.tensor_copy(out=KT[0:64, 2 * hp, key0:key0 + ntok], in_=pa[0:64, 0:ntok]),
              [ka], ktk(2 * hp, key0, key0 + ntok), n=ntok)
            A("dve", lambda e, hp=hp, pa=pa: e.tensor_copy(out=KT[0:64, 2 * hp + 1, key0:key0 + ntok], in_=pa[64:128, 0:ntok]),
              [ka], ktk(2 * hp + 1, key0, key0 + ntok), n=ntok)
        if bcast:
            dma(lambda e: e.dma_start(out=KT[64:96, :, key0:key0 + ntok],
                                      in_=kr_src[64:96, 0:ntok].unsqueeze(1).broadcast_to([32, NH, ntok])),
                [kr_key], [k_ for h in range(NH) for k_ in ktk(h, key0, key0 + ntok)], bsem)
        else:
            for h in range(NH):
                dma(lambda e, h=h: e.dma_start(out=KT[64:96, h, key0:key0 + ntok], in_=kr_src[64:96, 0:ntok]),
                    [kr_key], ktk(h, key0, key0 + ntok), f"KTr{h}")
        nblk = (ntok + 127) // 128
        for b in range(nblk):
            n = min(128, ntok - b * 128)
            kb = vb0 + b
            pa, ka = (nB() if KV_PSB else nA())
            A("pe", lambda e, b=b, n=n, pa=pa: e.matmul(pa[0:n, 0:512], lhsT=src_T[:, b * 128:b * 128 + n], rhs=Wkv[:, 512:1024],
                                                        start=True, stop=True),
              ["Wkv", src_key], [ka], cost=260)
            A("dve", lambda e, kb=kb, n=n, pa=pa: e.tensor_copy(out=V[0:n, kb, :], in_=pa[0:n, 0:512]), [ka], [f"V{kb}"], n=512)

    def attention(nq, q_lo, blocks, sg_rows, gsz=1):
        groups = [blocks[i:i + gsz] for i in range(0, len(blocks), gsz)]
        if gsz > 1 and len(groups[-1]) > 1 and groups[-1][-1][2] != 128:
            last = groups[-1].pop()
            groups.append([last])
        nb = len(blocks)
        for h in range(NH):
            hp = h % 2
            oi = h % 2
            pO = psO[:, oi * 512:(oi + 1) * 512]
            kO = f"psO{oi}"
            bi0 = 0
            for grp in groups:
                q0 = grp[0][3]
                diag = grp[0][4]
                n = nq - q0
                ng = len(grp)
                nkm = max(b_[2] for b_ in grp)
                pa, ka = nA()

                def smm(e, h=h, grp=grp, q0=q0, n=n, pa=pa):
                    ins = None
                    for i, (kc0, kb, nk, _q, _d) in enumerate(grp):
                        ins = e.matmul(pa[0:nk, i * n:(i + 1) * n], lhsT=KT[0:96, h, kc0:kc0 + nk], rhs=QT[0:96, h, q_lo + q0:q_lo + nq],
                                       start=True, stop=True)
                    return ins
                kreads = []
                for (kc0, kb, nk, _q, _d) in grp:
                    for k_ in ktk(h, kc0, kc0 + nk):
                        if k_ not in kreads:
                            kreads.append(k_)
                A("pe", smm, kreads + [f"QT{h}n", f"QT{h}r"], [ka], cost=ng * max(n * 0.55 + 30, 70))
                pi = nxt("PT", NPT)
                Pt, kP = PT[pi], f"PT{pi}"
                A("act", lambda e, nkm=nkm, n=n, ng=ng, pa=pa, Pt=Pt: e.activation(out=Pt[0:nkm, 0:ng * n], in_=pa[0:nkm, 0:ng * n], func=AF.Exp,
                                                                                   scale=ATT_SCALE),
                  [ka], [kP], n=ng * n)
                if diag:
                    A("pool", lambda e, Pt=Pt: e.memset(Pt[64:128, 0:64], 0.0), [], [kP], cost=200)

                def pv(e, h=h, grp=grp, q0=q0, n=n, Pt=Pt, bi0=bi0, pO=pO):
                    ins = None
                    for i, (kc0, kb, nk, _q, _d) in enumerate(grp):
                        bi = bi0 + i
                        e.matmul(pO[0:64, q0:nq], lhsT=V[0:nk, kb, h * 64:(h + 1) * 64], rhs=Pt[0:nk, i * n:(i + 1) * n],
                                 start=(bi == 0), stop=(bi == nb - 1))
                        ins = e.matmul(pO[64:128, q0:nq], lhsT=vones[0:nk, :], rhs=Pt[0:nk, i * n:(i + 1) * n],
                                       start=(bi == 0), stop=(bi == nb - 1))
                    return ins
                A("pe", pv, [f"V{b_[1]}" for b_ in grp] + ["vones", kP], [kO], cost=ng * max(n * 0.6 + 40, 100))
                bi0 += ng
            lo, hi_ = hp * 64, hp * 64 + 64
            krl, ko1 = f"rl{hp}", f"o1{hp}"
            A("dve", lambda e, pO=pO, lo=lo, hi_=hi_: e.reciprocal(out=rlb[lo:hi_, 0:nq], in_=pO[64:128, 0:nq]), [kO], [krl],
              cost=120 + 4.4 * nq)
            A("dve", lambda e, pO=pO, lo=lo, hi_=hi_: e.tensor_tensor(out=o1b[lo:hi_, 0:nq], in0=pO[0:64, 0:nq], in1=rlb[lo:hi_, 0:nq],
                                                                       op=ALU.mult),
              [kO, krl], [ko1], n=nq)
            A("pool", lambda e, h=h, lo=lo, hi_=hi_: e.tensor_tensor(
                out=mixedT[lo:hi_, 4 + h // 2, q_lo:q_lo + nq], in0=o1b[lo:hi_, 0:nq], in1=sga[lo:hi_, h // 2, sg_rows], op=ALU.mult),
              [ko1, "sga"], [f"mixedT{4 + h // 2}"], n=nq)

    gbanks = [(psA[0], "psA0"), (psA[1], "psA1"), (psO[:, 0:512], "psO0"), (psO[:, 512:1024], "psO1")]

    def stage_G(nsub, x_src, y_dst):
        for sub in range(nsub):
            ts = slice(sub * 128, (sub + 1) * 128)
            xi = nxt("xr", 2)
            dma(lambda e, xi=xi, sub=sub: e.dma_start(out=xres[xi][:], in_=x_src(sub)), [], [f"xres{xi}"], f"xres{xi}")
            g0 = nxt("G", 2) * 2
            bks = [gbanks[g0], gbanks[g0 + 1]]
            for half in range(2):
                pb, kb = bks[half]

                def mmo(e, ts=ts, half=half, pb=pb):
                    ins = None
                    for k in range(8):
                        ins = e.matmul(pb[:, 0:512], lhsT=mixedT[:, k, ts], rhs=Wout[:, k, half * 512:(half + 1) * 512],
                                       start=(k == 0), stop=(k == 7))
                    return ins
                A("pe", mmo, [f"mixedT{k}" for k in range(8)] + WOUT_KEYS, [kb], cost=8 * 250)
                A("act", lambda e, half=half, pb=pb: e.activation(out=junk[:, half * 512:(half + 1) * 512], in_=pb[:, 0:512], func=AF.Square,
                                                                  scale=1.0 / 32.0, accum_out=st[:, 3 + half:4 + half]),
                  [kb], ["junk", f"st{3 + half}"], n=512)
            A("pool", lambda e: e.tensor_tensor(out=st[:, 3:4], in0=st[:, 3:4], in1=st[:, 4:5], op=ALU.add), ["st3", "st4"], ["st3"], cost=300)
            A("pool", lambda e: e.tensor_scalar(out=st[:, 3:4], in0=st[:, 3:4], scalar1=EPS, scalar2=None, op0=ALU.add),
              ["st3"], ["st3"], cost=300)
            A("pool", lambda e: e.tensor_tensor(out=st[:, 3:4], in0=st[:, 3:4], in1=neghalf[:, 0:1], op=ALU.pow),
              ["st3", "neghalf"], ["st3"], cost=550)
            for half in range(2):
                pb, kb = bks[half]
                hs = slice(half * 512, (half + 1) * 512)
                A("dve", lambda e, pb=pb, hs=hs: e.scalar_tensor_tensor(out=tmpo[:, hs], in0=pb[:, 0:512], scalar=st[:, 3:4], in1=gpostb[:, hs],
                                                                        op0=ALU.mult, op1=ALU.mult),
                  [kb, "st3", "gpostb"], [f"tmpo{half}"], n=512)
            A("pool", lambda e, xi=xi: e.tensor_tensor(out=xres[xi][:], in0=tmpo[:], in1=xres[xi][:], op=ALU.add),
              ["tmpo0", "tmpo1", f"xres{xi}"], [f"xres{xi}"], n=1024)
            dma(lambda e, xi=xi, sub=sub: e.dma_start(out=y_dst(sub), in_=xres[xi][:]), [f"xres{xi}"], [], f"xresst{xi}")

    def sample_phase():
        NS = 128
        stage_AB(lambda sub: xs[:, :], 1, lambda sub: 16, lambda sub: ockv_s[:, :], lambda sub: okr_s[:, :])
        for s in range(NSEQ_S):
            dma(lambda e, s=s: e.dma_start(out=osb[0:30, :], in_=sconv[s, :, :]), [], ["osb"], "hist")
            pB, kB = nB()

            def trh(e, pB=pB):
                ins = None
                for c in range(4):
                    ins = e.transpose(out=pB[:, c * 32:c * 32 + 30], in_=osb[0:30, c * 128:(c + 1) * 128], identity=identf[0:30, 0:30])
                return ins
            A("pe", trh, ["osb", "identf"], [kB])
            A("dve", lambda e, s=s, pB=pB: e.tensor_copy(out=gluS[:, :, s, 0:30],
                                                         in_=pB[:, 0:128].rearrange("p (c n) -> p c n", c=4)[:, :, 0:30]),
              [kB], GLUK)
        stage_C1(NS,
                 lambda c: [((slice(s * 32, (s + 1) * 32), gluS[:, c, s, 30:62]), [f"glu{c}"]) for s in range(NSEQ_S)],
                 lambda c: (slice(0, NS), glu32[:, c, :]))

        def conv_mm_s(e, G, c, pB):
            ins = None
            for q in range(NSEQ_S):
                for r in range(8):
                    for g in range(4):
                        ins = e.matmul(pB[g * 32:(g + 1) * 32, q * 32:(q + 1) * 32], lhsT=Wp[:, c, g, r, :],
                                       rhs=G[:, g, q * 39 + r:q * 39 + r + 32], start=(r == 0), stop=(r == 7),
                                       tile_position=(0, g * 32))
            return ins
        stage_D(NS, conv_mm_s,
                lambda G, sh, part: G[sh * 32:(sh + 1) * 32, :, part * 39:(part + 1) * 39],
                lambda D, sh, part: D[:, :, part * 64 + 8 * sh:part * 64 + 8 * sh + 39],
                NSEQ_S * 8 * 80,
                lambda c: gluS[:, c, :, :].rearrange("p s t -> p (s t)"), NSEQ_S * 64, NSEQ_S)
        pB, kB = nB()

        def trgs(e, pB=pB):
            ins = None
            for c in range(4):
                ins = e.transpose(out=pB[:, c * 128:(c + 1) * 128], in_=glu32[:, c, :], identity=identf[:])
            return ins
        A("pe", trgs, ["glu32", "identf"], [kB])
        A("dve", lambda e, pB=pB: e.tensor_copy(out=osb[:], in_=pB[:, :]), [kB], ["osb"])
        for s in range(NSEQ_S):
            dma(lambda e, s=s: e.dma_start(out=oconv_s[s, :, :], in_=osb[s * 32 + 2:s * 32 + 32, :]), ["osb"], [], "osbst")
        gates(12, NS, sga, "sga")
        stage_E_q(NS, SEQ)
        for s in range(NSEQ_S):
            cbase, vbase = (s % 2) * 1056, (s % 2) * 9
            for half in range(2):
                fc, kfc = nfp()
                fr, kfr = nfp()
                dma(lambda e, s=s, half=half, fc=fc: e.dma_start(out=fc[:, 0:512].rearrange("p (b f) -> p b f", b=4),
                                                                in_=cckv[s, half * 512:(half + 1) * 512, :].rearrange("(b p) f -> p b f", p=128)),
                    [], [kfc], f"ld{kfc}")
                dma(lambda e, s=s, half=half, fr=fr: e.dma_start(out=fr[:, 0:384].rearrange("p (b f) -> p b f", b=4)[:, :, 64:96],
                                                                in_=ckr[s, half * 512:(half + 1) * 512, :].rearrange("(b p) f -> p b f", p=128)),
                    [], [kfr], f"ld{kfr}")
                for b4 in range(4):
                    pB, kB = nB()

                    def trk2(e, b4=b4, pB=pB, fc=fc, fr=fr):
                        e.transpose(out=pB[:, 0:128], in_=fc[:, b4 * 128:(b4 + 1) * 128], identity=identf[:])
                        return e.transpose(out=pB[0:96, 128:256], in_=fr[:, b4 * 96:(b4 + 1) * 96], identity=identf[:])
                    A("pe", trk2, [kfc, kfr, "identf"], [kB], cost=300)
                    ts = slice(b4 * 128, (b4 + 1) * 128)
                    A("dve", lambda e, ts=ts, pB=pB, half=half: e.tensor_copy(out=hT[:, half, ts], in_=pB[:, 0:128]), [kB], ["hT", f"cT{half}"], n=128)
                    A("dve", lambda e, ts=ts, pB=pB, half=half: e.tensor_copy(out=hT[64:96, 2 + half, ts], in_=pB[64:96, 128:256]), [kB], ["hT", f"cR{half}"], n=128)
                kv_project(512, cbase + half * 512, hT[:, half, :], f"cT{half}", hT[:, 2 + half, :], f"cR{half}", vb0=vbase + half * 4, bcast=True, bsem=f"KTrb{s % 2}{half}")
            kv_project(32, cbase + 1024, ckvT[:, s * 32:(s + 1) * 32], "ckvT", krT[:, s * 32:(s + 1) * 32], "krT", vb0=vbase + 8, bcast=True, bsem=f"KTrb{s % 2}n")
            blocks = [(cbase + kb * 128, vbase + kb, 128, 0, False) for kb in range(8)] + [(cbase + 1024, vbase + 8, 32, 0, False)]
            import os as _os
            if _os.environ.get("DBG_BLOCKS") == "cached":
                blocks = blocks[:8]
            elif _os.environ.get("DBG_BLOCKS") == "new":
                blocks = blocks[8:]
            elif _os.environ.get("DBG_BLOCKS") == "first":
                blocks = blocks[:1]
            attention(32, s * 32, blocks, slice(s * 32, (s + 1) * 32), gsz=4)
        if debug:
            dbg = nc.dram_tensor("dbg", [128, 8, 512], BF16, kind="ExternalOutput").ap()
            import os as _os
            if _os.environ.get("DBG_DUMP") == "hT":
                dma(lambda e: e.dma_start(out=dbg[:, :, :], in_=hT[:]), ["hT", "cT0", "cT1", "cR0", "cR1"], [], "dbg")
            elif _os.environ.get("DBG_DUMP") == "KT":
                dma(lambda e: e.dma_start(out=dbg[:, :, :], in_=KT[:, :, 0:512]), [f"KT{h}" for h in range(8)], [], "dbg")
            elif _os.environ.get("DBG_DUMP") == "V":
                dma(lambda e: e.dma_start(out=dbg[:, :, :], in_=V[:, 0:8, :]), [f"V{h}" for h in range(8)], [], "dbg")
            else:
                dma(lambda e: e.dma_start(out=dbg[:, :, :], in_=mixedT[:]), [f"mixedT{k}" for k in range(8)], [], "dbg")
        stage_G(1, lambda sub: xs[:, :], lambda sub: ys[:, :])


    def ab_prompt(p, t):
        t0 = t * TILE
        stage_AB(lambda sub, p=p, t0=t0: xp[p, t0 + sub * 128:t0 + (sub + 1) * 128, :], 4,
                 lambda sub, t=t: 4 * t + sub,
                 lambda sub, p=p, t0=t0: ockv_p[p, t0 + sub * 128:t0 + (sub + 1) * 128, :],
                 lambda sub, p=p, t0=t0: okr_p[p, t0 + sub * 128:t0 + (sub + 1) * 128, :])

    if do_prompt:
        ab_prompt(0, 0)
    setup_rest()
    if SAMPLE_POS == 'start':
        sample_phase()
    for p in range(NSEQ_P if do_prompt else 0):
        if p == 1 and SAMPLE_POS == 'mid':
            sample_phase()
        A("pool", lambda e: e.memset(glu[:, :, 0:30], 0.0), [], GLUK)
        for t in range(SEQ // TILE):
            t0 = t * TILE
            if (p, t) != (0, 0):
                ab_prompt(p, t)
            _unused = (lambda sub, p=p, t0=t0: xp[p, t0 + sub * 128:t0 + (sub + 1) * 128, :], 4,
                     lambda sub, t=t: 4 * t + sub,
                     lambda sub, p=p, t0=t0: ockv_p[p, t0 + sub * 128:t0 + (sub + 1) * 128, :],
                     lambda sub, p=p, t0=t0: okr_p[p, t0 + sub * 128:t0 + (sub + 1) * 128, :])
            last = (t == SEQ // TILE - 1)
            stage_C1(TILE,
                     lambda c: [((slice(0, TILE), glu[:, c, 30:30 + TILE]), [f"glu{c}"])],
                     (lambda c: (slice(TILE - 128, TILE), glu32[:, c, :])) if last else None)

            def conv_mm(e, G, c, pB):
                ins = None
                for r in range(8):
                    for g in range(4):
                        ins = e.matmul(pB[g * 32:(g + 1) * 32, 0:TILE], lhsT=Wp[:, c, g, r, :], rhs=G[:, g, r:r + TILE],
                                       start=(r == 0), stop=(r == 7), tile_position=(0, g * 32))
                return ins
            stage_D(TILE, conv_mm,
                    lambda G, sh, part: G[sh * 32:(sh + 1) * 32, :, 0:519],
                    lambda D, sh, part: D[:, :, 8 * sh:8 * sh + 519],
                    2750,
                    lambda c: glu[:, c, :], 544, 1)
            A("pool", lambda e: e.tensor_copy(out=glu[:, :, 0:30], in_=glu[:, :, TILE:TILE + 30]), GLUK, GLUK)
            if last:
                pB, kB = nB()

                def trg(e, pB=pB):
                    ins = None
                    for c in range(4):
                        ins = e.transpose(out=pB[:, c * 128:(c + 1) * 128], in_=glu32[:, c, :], identity=identf[:])
                    return ins
                A("pe", trg, ["glu32", "identf"], [kB])
                A("dve", lambda e, pB=pB: e.tensor_copy(out=osb[:], in_=pB[:, :]), [kB], ["osb"])
                dma(lambda e, p=p: e.dma_start(out=oconv_p[p, :, :], in_=osb[98:128, :]), ["osb"], [], "osbst")
            gates(12, TILE, sga, "sga")
            stage_E_q(TILE, t0)
            kv_project(TILE, t0, ckvT, "ckvT", krT, "krT")
            blocks = []
            for kb in range(4 * t + 4):
                j = kb - 4 * t
                if j < 0:
                    blocks.append((kb * 128, kb, 128, 0, False))
                else:
                    blocks.append((kb * 128, kb, 128, 128 * j, True))
            attention(TILE, 0, blocks, slice(0, TILE))
            stage_G(4, lambda sub, p=p, t0=t0: xp[p, t0 + sub * 128:t0 + (sub + 1) * 128, :],
                    lambda sub, p=p, t0=t0: yp[p, t0 + sub * 128:t0 + (sub + 1) * 128, :])

    if SAMPLE_POS == 'end':
        sample_phase()

    S.resolve()
    sems = {}
    for en in ("pe", "act", "dve", "pool"):
        sems[en] = es.enter_context(nc.semaphore(f"sem_{en}"))
    dma_sems = {}
    for k in S.dma_count:
        dma_sems[k] = es.enter_context(nc.semaphore(f"sd_{k}"))
    with es:
        with nc.Block() as block:
            @block.sync
            def _(e):
                S.run_engine("sp", e, sems, dma_sems, final_wait=True)

            @block.tensor
            def _(e):
                S.run_engine("pe", e, sems, dma_sems)

            @block.scalar
            def _(e):
                S.run_engine("act", e, sems, dma_sems)

            @block.vector
            def _(e):
                S.run_engine("dve", e, sems, dma_sems)

            @block.gpsimd
            def _(e):
                S.run_engine("pool", e, sems, dma_sems)
    return nc


def _rope_tables():
    inv = (10000.0 ** (-np.arange(0, 32, 2, dtype=np.float32) / 32.0)).astype(np.float32)
    pos = np.arange(SEQ, dtype=np.float32)
    ang = (pos[:, None] * inv[None, :]).astype(np.float32)
    cos = np.cos(ang).astype(np.float32)
    sin = np.sin(ang).astype(np.float32)
    cos_full = np.concatenate([cos, cos], axis=1)
    sin_sgn = np.concatenate([-sin, sin], axis=1)
    spos = PAST + (np.arange(128) % DEC_SEQ)
    costm = np.zeros((128, 17, 32), np.float32)
    sintm = np.zeros((128, 17, 32), np.float32)
    for i in range(16):
        costm[:, i, :] = cos_full[i * 128:(i + 1) * 128]
        sintm[:, i, :] = sin_sgn[i * 128:(i + 1) * 128]
    costm[:, 16, :] = cos_full[spos]
    sintm[:, 16, :] = sin_sgn[spos]
    cosT = np.concatenate([cos_full.T, cos_full[spos].T], axis=1)
    sinT = np.concatenate([sin_sgn.T, sin_sgn[spos].T], axis=1)
    return (np.ascontiguousarray(costm), np.ascontiguousarray(sintm),
            np.ascontiguousarray(cosT), np.ascontiguousarray(sinT))


_NC_CACHE = {}


def kernel(x_prompt, x_sample, cache_ckv, cache_krope, state_conv, g_pre, w_in, conv_w, conv_b,
           conv_ln_g, conv_ln_b, g_qa, w_qb, g_kva, w_kvb, w_out, g_post):
    f = lambda a: np.asarray(a, dtype=np.float32)
    x_prompt, x_sample = f(x_prompt), f(x_sample)
    cache_ckv, cache_krope, state_conv = f(cache_ckv)[0], f(cache_krope)[0], f(state_conv)[0]
    g_pre, w_in, conv_w, conv_b = f(g_pre)[0], f(w_in)[0], f(conv_w)[0], f(conv_b)[0]
    conv_ln_g, conv_ln_b, g_qa, w_qb = f(conv_ln_g)[0], f(conv_ln_b)[0], f(g_qa)[0], f(w_qb)[0]
    g_kva, w_kvb, w_out, g_post = f(g_kva)[0], f(w_kvb)[0], f(w_out)[0], f(g_post)[0]

    def pk(w, kc):
        return np.ascontiguousarray(w.reshape(kc, 128, w.shape[1]).transpose(1, 0, 2))

    w_a, w_b, w_gc = w_in[:, 0:512], w_in[:, 512:1024], w_in[:, 1024:1536]
    w_q, w_kvc, w_kr, w_ga = w_in[:, 1536:1792], w_in[:, 1792:1920], w_in[:, 1920:1952], w_in[:, 1952:2464]
    rot = np.concatenate([np.arange(16, 32), np.arange(0, 16)])
    wbig = pk(np.concatenate([w_a, w_b, w_gc, w_ga], axis=1), 8)
    wsmall = pk(np.concatenate([w_q, w_kvc, w_kr, w_kr[:, rot]], axis=1), 8)
    qh = w_qb.reshape(256, NH, 96)
    q0 = qh
    q1 = np.concatenate([qh[:, :, 0:64], qh[:, :, 64:96][:, :, rot]], axis=2)
    wqb = pk(np.stack([q0, q1], axis=2).reshape(256, NH * 2 * 96), 2)
    kvh = w_kvb.reshape(128, NH, 128)
    wkv = np.ascontiguousarray(np.concatenate([kvh[:, :, 0:64].reshape(128, 512), kvh[:, :, 64:128].reshape(128, 512)], axis=1))
    wout = pk(w_out, 8)
    cwpad = np.concatenate([conv_w, np.zeros((1, 512), np.float32)], axis=0)
    convwp = np.ascontiguousarray(cwpad.reshape(4, 8, 4, 4, 32).transpose(0, 4, 2, 3, 1).reshape(128, 4, 4, 8))
    estack = np.ascontiguousarray(np.tile(np.eye(32, dtype=np.float32), (4, 1)))
    vecs = np.zeros((128, 32), np.float32)
    vecs[:, 0:8] = g_pre.reshape(8, 128).T
    vecs[:, 8:10] = g_qa.reshape(2, 128).T
    vecs[:, 10:14] = conv_b.reshape(4, 128).T
    vecs[:, 14:18] = conv_ln_g.reshape(4, 128).T
    vecs[:, 18:22] = conv_ln_b.reshape(4, 128).T
    gpost_b = np.ascontiguousarray(np.broadcast_to(g_post[None, :], (128, 1024)))
    gkva_b = np.ascontiguousarray(np.broadcast_to(g_kva[None, :], (128, 128)))
    ident = np.eye(128, dtype=np.float32)
    costm, sintm, cosT, sinT = _rope_tables()

    if "nc" not in _NC_CACHE:
        _NC_CACHE["nc"] = build_program()
    nc = _NC_CACHE["nc"]

    in_maps = []
    for c in range(NCORES):
        in_maps.append({
            "xp": np.ascontiguousarray(x_prompt[c * NSEQ_P:(c + 1) * NSEQ_P]),
            "xs": np.ascontiguousarray(x_sample[c * NSEQ_S:(c + 1) * NSEQ_S].reshape(128, D_MODEL)),
            "cckv": np.ascontiguousarray(cache_ckv[c * NSEQ_S:(c + 1) * NSEQ_S]),
            "ckr": np.ascontiguousarray(cache_krope[c * NSEQ_S:(c + 1) * NSEQ_S]),
            "sconv": np.ascontiguousarray(state_conv[c * NSEQ_S:(c + 1) * NSEQ_S]),
            "wbig": wbig, "wsmall": wsmall, "wqb": wqb, "wkv": wkv, "wout": wout, "convwp": convwp, "estack": estack,
            "vecs": vecs, "gpost_b": gpost_b, "gkva_b": gkva_b, "ident": ident,
            "costm": costm, "sintm": sintm, "cosT": cosT, "sinT": sinT,
        })
    res = run_bass_kernel_spmd(nc, in_maps, core_ids=list(range(NCORES)))
    R = res.results
    cat = lambda k: np.concatenate([np.asarray(r[k], dtype=np.float32) for r in R], axis=0)
    y_prompt = cat("yp")
    y_sample = cat("ys").reshape(32, DEC_SEQ, D_MODEL)
    new_ckv_p = cat("ockv_p")[None]
    new_kr_p = cat("okr_p")[None]
    new_conv_p = cat("oconv_p")[None]
    new_ckv_s = cat("ockv_s").reshape(32, DEC_SEQ, 128)[None]
    new_kr_s = cat("okr_s").reshape(32, DEC_SEQ, 32)[None]
    new_conv_s = cat("oconv_s")[None]
    return (y_prompt, y_sample, new_ckv_p, new_kr_p, new_conv_p, new_ckv_s, new_kr_s, new_conv_s)
```
